# Optimizing a Trainium2 kernel written in Bass

```python
import math
import jax, jax.numpy as jnp
from jax import lax
import numpy as np

D_MODEL = 4096
BATCH = 4
SEQ = 4096
DEPTH = 2

D_FF = ((8 * D_MODEL // 3 + 255) // 256) * 256
SSM_WIDTH = D_MODEL // 4
SSM_GROUP = 16
SSM_GROUPS = SSM_WIDTH // SSM_GROUP
SSM_STATE = 64
RET_WIDTH = 3 * D_MODEL // 8
RET_KEY_DIM = 128
RET_VAL_DIM = 2 * RET_KEY_DIM
RET_HEADS = RET_WIDTH // RET_VAL_DIM
RET_CHUNK = 128
DIFF_WIDTH = 3 * D_MODEL // 8
DIFF_HEAD_DIM = 64
DIFF_VAL_DIM = 2 * DIFF_HEAD_DIM
DIFF_HEADS = DIFF_WIDTH // DIFF_VAL_DIM
Q_BLOCK = 128
N_BRANCH = 3
NORM_EPS = 1e-6
IN_SIZES = (SSM_WIDTH, RET_HEADS * RET_KEY_DIM, RET_HEADS * RET_KEY_DIM, RET_WIDTH, RET_WIDTH,
            DIFF_HEADS * 2 * DIFF_HEAD_DIM, DIFF_HEADS * 2 * DIFF_HEAD_DIM, DIFF_WIDTH, N_BRANCH * D_MODEL)
N_IN = sum(IN_SIZES)

kernel_name = "hybrid_s5_retention_diffattn_macaron"


def _split_points():
    pts, acc = [], 0
    for s in IN_SIZES[:-1]:
        acc += s
        pts.append(acc)
    return pts


def _rms(x):
    xf = x.astype(jnp.float32)
    return xf * lax.rsqrt(jnp.mean(xf * xf, axis=-1, keepdims=True) + NORM_EPS)


def rms_norm(x, gain):
    return (_rms(x) * gain.astype(jnp.float32)).astype(x.dtype)


def swiglu(x, w_in, w_out):
    gate, up = jnp.split(x @ w_in, 2, axis=-1)
    return (jax.nn.silu(gate) * up) @ w_out


def alibi_slopes(n_heads):
    return jnp.asarray(2.0 ** (-8.0 * np.arange(1, n_heads + 1) / n_heads), dtype=jnp.float32)


def s5_branch(u, lam_re, lam_im, log_dt, b_re, b_im, c_re, c_im, d_skip, glu_w, glu_b):
    f32 = jnp.float32
    bsz, seq, _ = u.shape
    ug = u.reshape(bsz, seq, SSM_GROUPS, SSM_GROUP).astype(f32)
    dt = jnp.exp(log_dt.astype(f32))[:, None]
    lr, li = lam_re.astype(f32), lam_im.astype(f32)
    mag = jnp.exp(lr * dt)
    ab_re, ab_im = mag * jnp.cos(li * dt), mag * jnp.sin(li * dt)
    den = lr * lr + li * li
    nr, ni = ab_re - 1.0, ab_im
    coef_re = (nr * lr + ni * li) / den
    coef_im = (ni * lr - nr * li) / den
    br, bi = b_re.astype(f32), b_im.astype(f32)
    bb_re = coef_re[..., None] * br - coef_im[..., None] * bi
    bb_im = coef_re[..., None] * bi + coef_im[..., None] * br
    bu_re = jnp.einsum('blgc,gpc->blgp', ug, bb_re)
    bu_im = jnp.einsum('blgc,gpc->blgp', ug, bb_im)
    a_re = jnp.broadcast_to(ab_re, (1, seq) + ab_re.shape)
    a_im = jnp.broadcast_to(ab_im, (1, seq) + ab_im.shape)

    def combine(e1, e2):
        a1r, a1i, b1r, b1i = e1
        a2r, a2i, b2r, b2i = e2
        return (a2r * a1r - a2i * a1i,
                a2r * a1i + a2i * a1r,
                a2r * b1r - a2i * b1i + b2r,
                a2r * b1i + a2i * b1r + b2i)

    _, _, xr, xi = lax.associative_scan(combine, (a_re, a_im, bu_re, bu_im), axis=1)
    y = (jnp.einsum('blgp,gcp->blgc', xr, c_re.astype(f32))
         - jnp.einsum('blgp,gcp->blgc', xi, c_im.astype(f32))
         + d_skip.astype(f32) * ug)
    y = jax.nn.gelu(y.reshape(bsz, seq, SSM_WIDTH).astype(u.dtype))
    return y * jax.nn.sigmoid(y @ glu_w + glu_b)


def retention_branch(q, k, v, g):
    bsz, seq = q.shape[0], q.shape[1]
    nc, c = seq // RET_CHUNK, RET_CHUNK
    dt = q.dtype
    log_gamma = jnp.asarray(np.log(1.0 - 2.0 ** (-5.0 - np.arange(RET_HEADS))), jnp.float32)
    pos = jnp.arange(c, dtype=jnp.float32)
    rel = pos[:, None] - pos[None, :]
    intra_decay = jnp.where(rel >= 0, jnp.exp(log_gamma[:, None, None] * jnp.maximum(rel, 0.0)), 0.0).astype(dt)
    key_decay = jnp.exp(log_gamma[:, None] * (c - 1 - pos)).astype(dt)
    query_decay = jnp.exp(log_gamma[:, None] * (pos + 1)).astype(dt)
    chunk_decay = jnp.exp(log_gamma * c).astype(dt)
    qc = q.reshape(bsz, nc, c, RET_HEADS, RET_KEY_DIM)
    kc = k.reshape(bsz, nc, c, RET_HEADS, RET_KEY_DIM) * (RET_KEY_DIM ** -0.5)
    vc = v.reshape(bsz, nc, c, RET_HEADS, RET_VAL_DIM)
    scores = jnp.einsum('bnchd,bnmhd->bnhcm', qc, kc) * intra_decay
    intra = jnp.einsum('bnhcm,bnmhe->bnche', scores, vc)
    kv = jnp.einsum('bnmhd,hm,bnmhe->nbhde', kc, key_decay, vc)

    def step(state, kv_chunk):
        return chunk_decay[None, :, None, None] * state + kv_chunk, state

    _, prev = lax.scan(step, jnp.zeros(kv.shape[1:], kv.dtype), kv)
    inter = jnp.einsum('bnchd,hc,nbhde->bnche', qc, query_decay, prev)
    o = _rms(intra + inter).astype(dt).reshape(bsz, seq, RET_WIDTH)
    return jax.nn.silu(g) * o


def diff_attention_branch(q, k, v, q_gain, k_gain, lq1, lk1, lq2, lk2, sub_gain, lambda_init):
    f32 = jnp.float32
    bsz, seq = q.shape[0], q.shape[1]
    nb = seq // Q_BLOCK
    q = rms_norm(q, q_gain) * (DIFF_HEAD_DIM ** -0.5)
    k = rms_norm(k, k_gain)
    lam = (jnp.exp(jnp.sum(lq1.astype(f32) * lk1.astype(f32)))
           - jnp.exp(jnp.sum(lq2.astype(f32) * lk2.astype(f32))) + lambda_init)
    q_blocks = q.reshape(bsz, nb, Q_BLOCK, DIFF_HEADS, 2, DIFF_HEAD_DIM).transpose(1, 0, 3, 4, 2, 5)
    k_t = k.transpose(0, 2, 3, 1, 4)
    v_t = v.reshape(bsz, seq, DIFF_HEADS, DIFF_VAL_DIM).transpose(0, 2, 1, 3)
    slopes = alibi_slopes(DIFF_HEADS)[:, None, None, None]
    key_pos = jnp.arange(seq)

    def attend(args):
        q_blk, start = args
        dist = ((start + jnp.arange(Q_BLOCK))[:, None] - key_pos[None, :]).astype(f32)
        s = jnp.einsum('bhiqd,bhisd->bhiqs', q_blk, k_t).astype(f32) - slopes * dist
        s = jnp.where(dist >= 0, s, -jnp.inf)
        p = jax.nn.softmax(s, axis=-1)
        w = (p[:, :, 0] - lam * p[:, :, 1]).astype(v_t.dtype)
        return jnp.einsum('bhqs,bhse->bqhe', w, v_t)

    out = lax.map(attend, (q_blocks, jnp.arange(nb) * Q_BLOCK))
    out = out.transpose(1, 0, 2, 3, 4).reshape(bsz, seq, DIFF_HEADS, DIFF_VAL_DIM)
    out = rms_norm(out, sub_gain) * (1.0 - lambda_init)
    return out.reshape(bsz, seq, DIFF_WIDTH)


def setup_inputs(seed: int = 0) -> dict:
    key = jax.random.key(seed)
    ks = jax.random.split(key, 26)
    f32 = jnp.float32
    L, G, P, C = DEPTH, SSM_GROUPS, SSM_STATE, SSM_GROUP

    def nrm(k, shape, scale):
        return jax.random.normal(k, shape, f32) * scale

    return {
        "x": jax.random.normal(ks[0], (BATCH, SEQ, D_MODEL), f32),
        "norm_w": 1.0 + nrm(ks[1], (L, 3, D_MODEL), 0.01),
        "ffn_w_in": nrm(ks[2], (L, 2, D_MODEL, 2 * D_FF), D_MODEL ** -0.5),
        "ffn_w_out": nrm(ks[3], (L, 2, D_FF, D_MODEL), D_FF ** -0.5),
        "mix_w_in": nrm(ks[4], (L, D_MODEL, N_IN), D_MODEL ** -0.5),
        "ssm_lambda_re": -0.5 + nrm(ks[5], (L, G, P), 0.01),
        "ssm_lambda_im": math.pi * jnp.arange(P, dtype=f32) + nrm(ks[6], (L, G, P), 0.01),
        "ssm_log_dt": jax.random.uniform(ks[7], (L, G), f32, math.log(1e-3), math.log(1e-1)),
        "ssm_b_re": nrm(ks[8], (L, G, P, C), (2 * C) ** -0.5),
        "ssm_b_im": nrm(ks[9], (L, G, P, C), (2 * C) ** -0.5),
        "ssm_c_re": nrm(ks[10], (L, G, C, P), (2 * P) ** -0.5),
        "ssm_c_im": nrm(ks[11], (L, G, C, P), (2 * P) ** -0.5),
        "ssm_d": nrm(ks[12], (L, G, C), 1.0),
        "ssm_glu_w": nrm(ks[13], (L, SSM_WIDTH, SSM_WIDTH), SSM_WIDTH ** -0.5),
        "ssm_glu_b": nrm(ks[14], (L, SSM_WIDTH), 0.01),
        "diff_q_gain": 1.0 + nrm(ks[15], (L, DIFF_HEAD_DIM), 0.01),
        "diff_k_gain": 1.0 + nrm(ks[16], (L, DIFF_HEAD_DIM), 0.01),
        "diff_lambda_q1": nrm(ks[17], (L, DIFF_HEAD_DIM), 0.1),
        "diff_lambda_k1": nrm(ks[18], (L, DIFF_HEAD_DIM), 0.1),
        "diff_lambda_q2": nrm(ks[19], (L, DIFF_HEAD_DIM), 0.1),
        "diff_lambda_k2": nrm(ks[20], (L, DIFF_HEAD_DIM), 0.1),
        "diff_sub_gain": 1.0 + nrm(ks[21], (L, DIFF_VAL_DIM), 0.01),
        "w_branch_ssm": nrm(ks[22], (L, SSM_WIDTH, D_MODEL), SSM_WIDTH ** -0.5),
        "w_branch_ret": nrm(ks[23], (L, RET_WIDTH, D_MODEL), RET_WIDTH ** -0.5),
        "w_branch_diff": nrm(ks[24], (L, DIFF_WIDTH, D_MODEL), DIFF_WIDTH ** -0.5),
        "w_out": nrm(ks[25], (L, D_MODEL, D_MODEL), D_MODEL ** -0.5),
    }


def reference(x, norm_w, ffn_w_in, ffn_w_out, mix_w_in, ssm_lambda_re, ssm_lambda_im, ssm_log_dt,
              ssm_b_re, ssm_b_im, ssm_c_re, ssm_c_im, ssm_d, ssm_glu_w, ssm_glu_b,
              diff_q_gain, diff_k_gain, diff_lambda_q1, diff_lambda_k1, diff_lambda_q2, diff_lambda_k2,
              diff_sub_gain, w_branch_ssm, w_branch_ret, w_branch_diff, w_out):
    bsz, seq, _ = x.shape
    h = x
    for layer in range(DEPTH):
        h = h + 0.5 * swiglu(rms_norm(h, norm_w[layer, 0]), ffn_w_in[layer, 0], ffn_w_out[layer, 0])
        xn = rms_norm(h, norm_w[layer, 1])
        u, rq, rk, rv, rg, dq, dk, dv, gates = jnp.split(xn @ mix_w_in[layer], _split_points(), axis=-1)
        y_ssm = s5_branch(u, ssm_lambda_re[layer], ssm_lambda_im[layer], ssm_log_dt[layer],
                          ssm_b_re[layer], ssm_b_im[layer], ssm_c_re[layer], ssm_c_im[layer],
                          ssm_d[layer], ssm_glu_w[layer], ssm_glu_b[layer])
        y_ret = retention_branch(rq.reshape(bsz, seq, RET_HEADS, RET_KEY_DIM),
                                 rk.reshape(bsz, seq, RET_HEADS, RET_KEY_DIM), rv, rg)
        lambda_init = 0.8 - 0.6 * math.exp(-0.3 * layer)
        y_diff = diff_attention_branch(dq.reshape(bsz, seq, DIFF_HEADS, 2, DIFF_HEAD_DIM),
                                       dk.reshape(bsz, seq, DIFF_HEADS, 2, DIFF_HEAD_DIM), dv,
                                       diff_q_gain[layer], diff_k_gain[layer],
                                       diff_lambda_q1[layer], diff_lambda_k1[layer],
                                       diff_lambda_q2[layer], diff_lambda_k2[layer],
                                       diff_sub_gain[layer], lambda_init)
        g = jax.nn.sigmoid(gates.reshape(bsz, seq, N_BRANCH, D_MODEL))
        merged = (g[:, :, 0] * (y_ssm @ w_branch_ssm[layer])
                  + g[:, :, 1] * (y_ret @ w_branch_ret[layer])
                  + g[:, :, 2] * (y_diff @ w_branch_diff[layer]))
        h = h + merged @ w_out[layer]
        h = h + 0.5 * swiglu(rms_norm(h, norm_w[layer, 2]), ffn_w_in[layer, 1], ffn_w_out[layer, 1])
    return h
```

```python
import math
import numpy as np
import concourse.bass as bass
import concourse.mybir as mybir
from concourse.bass_utils import run_bass_kernel_spmd

F32 = mybir.dt.float32
BF16 = mybir.dt.bfloat16
AF = mybir.ActivationFunctionType
ALU = mybir.AluOpType

NORM_EPS = 1e-6
TT = 512


class Res:
    __slots__ = ("name", "last_w", "readers")

    def __init__(self, name):
        self.name = name
        self.last_w = None
        self.readers = {}


class Ctx:
    def __init__(self):
        nc = self.nc = bass.Bass("TRN2", target_bir_lowering=False)
        self.eng = {"pe": nc.tensor, "act": nc.scalar, "dve": nc.vector, "pool": nc.gpsimd, "sp": nc.sync}
        self.semobj = {}
        self.cnt = {}
        for e in self.eng:
            self.semobj[e] = nc.alloc_semaphore("s_" + e)
            self.cnt[e] = 0
        self.known = {e: {} for e in self.eng}
        self.nsem = 0

    def newsem(self):
        k = "d%d" % self.nsem
        self.nsem += 1
        self.semobj[k] = self.nc.alloc_semaphore(k)
        self.cnt[k] = 0
        return k

    def sb(self, name, shape, dt):
        return self.nc.alloc_sbuf_tensor(name, shape, dt)

    def ps(self, name, shape, dt=F32):
        return self.nc.alloc_psum_tensor(name, shape, dt)

    def _wait(self, e, dep):
        if dep is None:
            return
        k, v = dep
        if self.known[e].get(k, 0) >= v:
            return
        if k == e and v > self.cnt[e]:
            return
        self.eng[e].wait_ge(self.semobj[k], v)
        self.known[e][k] = v

    def _deps(self, e, reads, writes):
        for r in reads:
            self._wait(e, r.last_w)
        for w in writes:
            self._wait(e, w.last_w)
            for k, v in w.readers.items():
                self._wait(e, (k, v))

    def op(self, e, fn, reads=(), writes=(), sig=True):
        self._deps(e, reads, writes)
        inst = fn(self.eng[e])
        if sig:
            self.cnt[e] += 1
            inst.then_inc(self.semobj[e], 1)
            c = self.cnt[e]
        else:
            c = self.cnt[e] + 1
        for r in reads:
            r.readers[e] = c
        for w in writes:
            w.last_w = (e, c)
            w.readers = {}
        return inst

    def dma(self, q, sem, out, in_, reads=(), writes=(), **kw):
        self._deps(q, reads, writes)
        inst = self.eng[q].dma_start(out=out, in_=in_, **kw)
        self.cnt[sem] += 16
        inst.then_inc(self.semobj[sem], 16)
        c = self.cnt[sem]
        for r in reads:
            r.readers[sem] = c
        for w in writes:
            w.last_w = (sem, c)
            w.readers = {}
        return inst

    def barrier(self):
        snap = dict(self.cnt)
        for e in self.eng:
            for k, v in snap.items():
                if k != e and v > 0:
                    self._wait(e, (k, v))

    def finish(self, outs):
        for r in outs:
            self._wait("sp", r.last_w)


class Ring:
    def __init__(self, c, name, n, shape, dt):
        self.bufs = [c.sb("%s%d" % (name, i), shape, dt) for i in range(n)]
        self.res = [Res("%s%d" % (name, i)) for i in range(n)]
        self.sems = [c.newsem() for _ in range(n)]
        self.i = 0
        self.n = n

    def next(self):
        i = self.i
        self.i = (i + 1) % self.n
        return self.bufs[i], self.res[i], self.sems[i]


class Model:
    def __init__(self, D, B, S, DEPTH, D_FF, debug=False):
        self.debug = debug
        self.D, self.B, self.S, self.DEPTH, self.D_FF = D, B, S, DEPTH, D_FF
        self.KC = D // 128
        self.HC = D_FF // 128
        self.NTOK = B * S
        self.NT = self.NTOK // TT
        c = self.c = Ctx()
        nc = self.nc = c.nc
        KC, HC = self.KC, self.HC
        self.xT = nc.dram_tensor("xT", [D, self.NTOK], F32, kind="ExternalInput").ap()
        self.normw = nc.dram_tensor("normw", [DEPTH * 3, 128, KC], F32, kind="ExternalInput").ap()
        self.ffn_w_in = nc.dram_tensor("ffn_w_in", [DEPTH, 2, 2 * D_FF // 256, 128, KC, 256], F32, kind="ExternalInput").ap()
        self.ffn_w_out = nc.dram_tensor("ffn_w_out", [DEPTH, 2, D // 256, 128, HC, 256], F32, kind="ExternalInput").ap()
        self.outT = nc.dram_tensor("outT", [D, self.NTOK], F32, kind="ExternalOutput").ap()
        self.r_out = Res("outT")
        self.s_out = c.newsem()
        self.hA = nc.dram_tensor("hA", [D, self.NTOK], F32, kind="Internal").ap()
        self.hB = nc.dram_tensor("hB", [D, self.NTOK], F32, kind="Internal").ap()
        self.r_hA, self.r_hB = Res("hA"), Res("hB")
        self.s_hA, self.s_hB = c.newsem(), c.newsem()
        self.r_xT = Res("xT")
        self.gains = c.sb("gains", [128, DEPTH * 3, KC], F32)
        self.r_gains = Res("gains")
        self.ones = c.sb("ones", [128, 128], F32)
        self.r_ones = Res("ones")
        self.xn_off = ((nc.sbuf_base + 31) // 32) * 32
        self.xn = c.sb("xn", [128, KC, TT], BF16)
        self.r_xn = Res("xn")
        self.hid_off = ((nc.sbuf_base + 31) // 32) * 32
        self.hid = c.sb("hid", [128, HC, TT], BF16)
        self.r_hid = Res("hid")
        self.rstd = c.sb("rstd", [128, TT], F32)
        self.r_rstd = Res("rstd")
        self.wring = Ring(c, "w", 3, [128, 32, 256], BF16)
        self.hring = Ring(c, "hc", 3, [128, TT], F32)
        self.sqring = Ring(c, "sq", 2, [128, TT], F32)
        self.tring = Ring(c, "tmp", 3, [128, TT], F32)
        self.oring = Ring(c, "o", 2, [128, TT], F32)
        self.macc = [c.sb("macc%d" % i, [128, TT], F32) for i in range(2)]
        self.r_macc = [Res("macc%d" % i) for i in range(2)]
        self.s_vload, self.s_kload, self.s_yload = c.newsem(), c.newsem(), c.newsem()
        self.pbank = [c.ps("pb%d" % i, [128, TT]) for i in range(8)]
        self.r_pb = [Res("pb%d" % i) for i in range(8)]
        c.op("dve", lambda e: e.memset(self.ones[:], 1.0), writes=[self.r_ones])
        sg = c.newsem()
        c.dma("sp", sg, self.gains[:], self.normw.rearrange("n p k -> p n k"), writes=[self.r_gains])

    def norm_tile(self, src, r_src, t, gidx):
        c, KC, D = self.c, self.KC, self.D
        ts = slice(t * TT, (t + 1) * TT)
        pb, rpb = self.pbank[7], self.r_pb[7]
        for kc in range(KC):
            hb, rh, sh = self.hring.next()
            c.dma("sp", sh, hb[:], src[kc * 128:(kc + 1) * 128, ts], reads=[r_src], writes=[rh])
            sq, rsq, _ = self.sqring.next()
            c.op("act", lambda e: e.activation(out=sq[:], in_=hb[:], func=AF.Square), reads=[rh], writes=[rsq])
            c.op("pe", lambda e: e.matmul(pb[:], lhsT=self.ones[:], rhs=sq[:], start=(kc == 0), stop=(kc == KC - 1)),
                 reads=[rsq, self.r_ones], writes=[rpb])
        tb, rt, _ = self.tring.next()
        c.op("dve", lambda e: e.tensor_scalar(out=tb[:], in0=pb[:], scalar1=1.0 / D, scalar2=NORM_EPS, op0=ALU.mult, op1=ALU.add),
             reads=[rpb], writes=[rt])
        tb2, rt2, _ = self.tring.next()
        c.op("act", lambda e: e.activation(out=tb2[:], in_=tb[:], func=AF.Sqrt), reads=[rt], writes=[rt2])
        c.op("dve", lambda e: e.reciprocal(out=self.rstd[:], in_=tb2[:]), reads=[rt2], writes=[self.r_rstd])
        for kc in range(KC):
            hb, rh, sh = self.hring.next()
            c.dma("sp", sh, hb[:], src[kc * 128:(kc + 1) * 128, ts], reads=[r_src], writes=[rh])
            c.op("dve", lambda e: e.scalar_tensor_tensor(out=self.xn[:, kc, :], in0=hb[:], scalar=self.gains[:, gidx, kc:kc + 1],
                                                         in1=self.rstd[:], op0=ALU.mult, op1=ALU.mult),
                 reads=[rh, self.r_rstd, self.r_gains], writes=[self.r_xn])

    def load_w(self, Wt, k0, kcn, n0, ncols):
        c = self.c
        assert n0 % 256 == 0 and ncols <= 256 and k0 % 128 == 0
        wb, rw, sw = self.wring.next()
        src = Wt[n0 // 256, :, k0 // 128:k0 // 128 + kcn, 0:ncols]
        c.dma("pool", sw, wb[:, 0:kcn, 0:ncols], src, writes=[rw])
        return wb, rw

    def ffn(self, src, r_src, dst, r_dst, s_dst, layer, idx):
        c, KC, HC, D, D_FF = self.c, self.KC, self.HC, self.D, self.D_FF
        W1 = self.ffn_w_in[layer, idx]
        W2 = self.ffn_w_out[layer, idx]
        gidx = layer * 3 + (0 if idx == 0 else 2)
        for t in range(self.NT):
            ts = slice(t * TT, (t + 1) * TT)
            self.norm_tile(src, r_src, t, gidx)
            for j0 in range(0, HC, 2):
                nj = min(2, HC - j0)
                wg, rwg = self.load_w(W1, 0, KC, j0 * 128, nj * 128)
                wu, rwu = self.load_w(W1, 0, KC, D_FF + j0 * 128, nj * 128)
                for jj in range(nj):
                    pg, rpg = self.pbank[jj], self.r_pb[jj]
                    for kc in range(KC):
                        c.op("pe", lambda e: e.matmul(pg[:], lhsT=wg[:, kc, jj * 128:(jj + 1) * 128], rhs=self.xn[:, kc, :],
                                                      start=(kc == 0), stop=(kc == KC - 1)), reads=[rwg, self.r_xn], writes=[rpg], sig=(kc == KC - 1))
                    tb, rt, _ = self.tring.next()
                    c.op("act", lambda e: e.activation(out=tb[:], in_=pg[:], func=AF.Silu), reads=[rpg], writes=[rt])
                    pu, rpu = self.pbank[2 + jj], self.r_pb[2 + jj]
                    for kc in range(KC):
                        c.op("pe", lambda e: e.matmul(pu[:], lhsT=wu[:, kc, jj * 128:(jj + 1) * 128], rhs=self.xn[:, kc, :],
                                                      start=(kc == 0), stop=(kc == KC - 1)), reads=[rwu, self.r_xn], writes=[rpu], sig=(kc == KC - 1))
                    c.op("dve", lambda e: e.tensor_tensor(out=self.hid[:, j0 + jj, :], in0=pu[:], in1=tb[:], op=ALU.mult),
                         reads=[rpu, rt], writes=[self.r_hid])
            for jo0 in range(0, KC, 2):
                for kb in range(0, HC, 32):
                    kcn = min(32, HC - kb)
                    w2, rw2 = self.load_w(W2, kb * 128, kcn, jo0 * 128, 256)
                    last_blk = (kb + kcn == HC)
                    for jj in range(2):
                        po, rpo = self.pbank[4 + jj], self.r_pb[4 + jj]
                        for k in range(kcn):
                            c.op("pe", lambda e: e.matmul(po[:], lhsT=w2[:, k, jj * 128:(jj + 1) * 128], rhs=self.hid[:, kb + k, :],
                                                          start=(kb == 0 and k == 0), stop=(last_blk and k == kcn - 1)),
                                 reads=[rw2, self.r_hid], writes=[rpo], sig=(k == kcn - 1))
                for jj in range(2):
                    po, rpo = self.pbank[4 + jj], self.r_pb[4 + jj]
                    jo = jo0 + jj
                    hb, rh, sh = self.hring.next()
                    c.dma("sp", sh, hb[:], src[jo * 128:(jo + 1) * 128, ts], reads=[r_src], writes=[rh])
                    ob, ro, so = self.oring.next()
                    c.op("dve", lambda e: e.scalar_tensor_tensor(out=ob[:], in0=po[:], scalar=0.5, in1=hb[:], op0=ALU.mult, op1=ALU.add),
                         reads=[rpo, rh], writes=[ro])
                    c.dma("sp", s_dst, dst[jo * 128:(jo + 1) * 128, ts], ob[:], reads=[ro], writes=[r_dst])


    def setup_mixer(self):
        c, nc, D, KC = self.c, self.nc, self.D, self.KC
        DEPTH, NTOK = self.DEPTH, self.NTOK
        self.SSMW = D // 4
        self.RH = (3 * D // 8) // 256
        self.RK = self.RH * 128
        self.RW = self.RH * 256
        self.DH = (3 * D // 8) // 128
        self.DK2 = self.DH * 128
        self.DW = self.DH * 128
        self.o_u = 0
        self.o_rq = self.SSMW
        self.o_rk = self.o_rq + self.RK
        self.o_rv = self.o_rk + self.RK
        self.o_rg = self.o_rv + self.RW
        self.o_dq = self.o_rg + self.RW
        self.o_dk = self.o_dq + self.DK2
        self.o_dv = self.o_dk + self.DK2
        self.o_g = self.o_dv + self.DW
        self.N_IN = self.o_g + 3 * D
        self.WTOT = self.SSMW + self.RW + self.DW
        self.mix_w_in = nc.dram_tensor("mix_w_in", [DEPTH, self.N_IN // 256, 128, KC, 256], F32, kind="ExternalInput").ap()
        self.w_b = [nc.dram_tensor("w_branch_ssm", [DEPTH, D // 256, 128, self.SSMW // 128, 256], F32, kind="ExternalInput").ap(),
                    nc.dram_tensor("w_branch_ret", [DEPTH, D // 256, 128, self.RW // 128, 256], F32, kind="ExternalInput").ap(),
                    nc.dram_tensor("w_branch_diff", [DEPTH, D // 256, 128, self.DW // 128, 256], F32, kind="ExternalInput").ap()]
        self.w_o = nc.dram_tensor("w_out", [DEPTH, D // 256, 128, KC, 256], F32, kind="ExternalInput").ap()
        self.dsmall = nc.dram_tensor("dsmall", [DEPTH, 128, 3 + 4 * 64], F32, kind="ExternalInput").ap()
        self.PF_secs = []
        for nm, r0, r1 in (("u", self.o_u, self.o_rq), ("rq", self.o_rq, self.o_rk), ("rk", self.o_rk, self.o_rv), ("rg", self.o_rg, self.o_dq),
                           ("dq", self.o_dq, self.o_dk), ("dk", self.o_dk, self.o_dv)):
            self.PF_secs.append((r0, r1, nc.dram_tensor("PF_" + nm, [r1 - r0, NTOK], F32, kind="Internal").ap()))
        self.VR = nc.dram_tensor("VR", [NTOK, self.RW], F32, kind="Internal").ap()
        self.VD = nc.dram_tensor("VD", [NTOK, self.DW], F32, kind="Internal").ap()
        self.KR = nc.dram_tensor("KR", [NTOK, self.RK], F32, kind="Internal").ap()
        kind = "ExternalOutput" if self.debug else "Internal"
        self.YT = nc.dram_tensor("YT", [self.WTOT, NTOK], F32, kind=kind).ap()
        self.r_PF, self.r_VR, self.r_VD, self.r_KR, self.r_YT = Res("PF"), Res("VR"), Res("VD"), Res("KR"), Res("YT")
        self.s_PF, self.s_VR, self.s_VD, self.s_KR, self.s_YT = (c.newsem() for _ in range(5))
        base = self.hid_off
        S = self.S
        off = [base]

        def at(name, shape, dt, esz):
            n = 1
            for d in shape[1:]:
                n *= d
            t = nc.alloc_sbuf_tensor_at(name, shape, dt, offset=off[0])
            off[0] += ((n * esz + 31) // 32) * 32
            return t
        self.a_q = at("a_q", [128, S], BF16, 2)
        self.a_k = at("a_k", [128, S], BF16, 2)
        self.a_qd = at("a_qd", [128, S], BF16, 2)
        self.a_v = at("a_v", [128, S // 128, 256], BF16, 2)
        self.a_kd = at("a_kd", [128, S // 128, 128], BF16, 2)
        self.a_kf = nc.alloc_sbuf_tensor_at("a_kf", [128, S // 128, 128], F32, offset=self.xn_off)
        assert (S // 128) * 128 * 4 <= self.KC * TT * 2
        self.a_R = at("a_R", [128, 5, TT], F32, 4)
        self.a_dec = at("a_dec", [128, 128], F32, 4)
        self.a_qdt = at("a_qdt", [128, TT], F32, 4)
        self.a_kdt = at("a_kdt", [128, 1], F32, 4)
        self.a_io = at("a_io", [128, TT], F32, 4)
        self.a_pi = at("a_pi", [128, 1], F32, 4)
        self.a_st = at("a_st", [128, 256], F32, 4)
        self.a_stb = at("a_stb", [128, 256], BF16, 2)
        self.a_pt = [at("a_pt%d" % i, [128, TT], BF16, 2) for i in range(3)]
        self.a_ds = at("a_ds", [128, 3 + 4 * 64], F32, 4)
        self.a_lam = at("a_lam", [128, 8], F32, 4)
        self.a_onesb = at("a_onesb", [128, 128], BF16, 2)
        self.a_blk = at("a_blk", [128, 128], F32, 4)
        self.a_i32 = at("a_i32", [128, TT], mybir.dt.int32, 4)
        assert off[0] - base <= self.HC * TT * 2, (off[0] - base, self.HC * TT * 2)
        self.r_ar = Res("arena")
        self.r_tab = Res("tables")
        self.r_pt = [Res("pt%d" % i) for i in range(3)]
        self.r_st, self.r_stb = Res("st"), Res("stb")
        self.pti = 0

    def pf(self, r0, r1):
        for a, b, t in self.PF_secs:
            if a <= r0 and r1 <= b:
                return t[r0 - a:r1 - a, :]
        raise ValueError((r0, r1))

    def mixer_consts(self):
        c = self.c
        I32 = mybir.dt.int32
        r = self.r_tab
        c.op("pool", lambda e: e.iota(self.a_i32[:], pattern=[[1, TT]], base=0, channel_multiplier=-1), writes=[r])
        c.op("dve", lambda e: e.tensor_copy(out=self.a_R[:, 0, :], in_=self.a_i32[:]), reads=[r], writes=[r])
        for j in range(4):
            c.op("pool", lambda e: e.affine_select(out=self.a_R[:, 1 + j, :], in_=self.a_R[:, 0, :], pattern=[[1, TT]],
                                                   compare_op=ALU.is_ge, fill=1.0e6, base=-128 * j, channel_multiplier=-1),
                 reads=[r], writes=[r])
        c.op("pool", lambda e: e.iota(self.a_i32[:], pattern=[[0, TT // 128], [1, 128]], base=0, channel_multiplier=0), reads=[r], writes=[r])
        c.op("dve", lambda e: e.tensor_copy(out=self.a_io[:], in_=self.a_i32[:]), reads=[r], writes=[r])
        c.op("pool", lambda e: e.iota(self.a_i32[:, 0:1], pattern=[[1, 1]], base=0, channel_multiplier=1), reads=[r], writes=[r])
        c.op("dve", lambda e: e.tensor_copy(out=self.a_pi[:], in_=self.a_i32[:, 0:1]), reads=[r], writes=[r])
        c.op("dve", lambda e: e.memset(self.a_onesb[:], 1.0), reads=[r], writes=[r])
        c.op("dve", lambda e: e.memset(self.a_blk[:], 0.0), reads=[r], writes=[r])
        c.op("dve", lambda e: e.memset(self.a_blk[0:64, 0:64], 1.0), reads=[r], writes=[r])
        c.op("dve", lambda e: e.memset(self.a_blk[64:128, 64:128], 1.0), reads=[r], writes=[r])

    def inproj(self, src, r_src, layer):
        c, KC = self.c, self.KC
        W = self.mix_w_in[layer]
        gidx = layer * 3 + 1
        nchunks = self.o_g // 128
        for t in range(self.NT):
            ts = slice(t * TT, (t + 1) * TT)
            self.norm_tile(src, r_src, t, gidx)
            for j0 in range(0, nchunks, 2):
                nj = min(2, nchunks - j0)
                wb, rw = self.load_w(W, 0, KC, j0 * 128, nj * 128)
                for jj in range(nj):
                    col = (j0 + jj) * 128
                    in_rv = self.o_rv <= col < self.o_rg
                    in_dv = self.o_dv <= col < self.o_g
                    in_rk = self.o_rk <= col < self.o_rv
                    if not (in_rv or in_dv):
                        pg, rpg = self.pbank[jj], self.r_pb[jj]
                        for kc in range(KC):
                            c.op("pe", lambda e: e.matmul(pg[:], lhsT=wb[:, kc, jj * 128:(jj + 1) * 128], rhs=self.xn[:, kc, :],
                                                          start=(kc == 0), stop=(kc == KC - 1)), reads=[rw, self.r_xn], writes=[rpg], sig=(kc == KC - 1))
                        ob, ro, so = self.oring.next()
                        c.op("act", lambda e: e.activation(out=ob[:], in_=pg[:], func=AF.Copy), reads=[rpg], writes=[ro])
                        c.dma("sp", self.s_PF, self.pf(col, col + 128)[:, ts], ob[:], reads=[ro], writes=[self.r_PF])
                    if in_rv or in_dv or in_rk:
                        if in_rv:
                            dst, rd, sd, cc = self.VR, self.r_VR, self.s_VR, col - self.o_rv
                        elif in_dv:
                            dst, rd, sd, cc = self.VD, self.r_VD, self.s_VD, col - self.o_dv
                        else:
                            dst, rd, sd, cc = self.KR, self.r_KR, self.s_KR, col - self.o_rk
                        for tb in range(TT // 128):
                            pg, rpg = self.pbank[2 + tb % 2], self.r_pb[2 + tb % 2]
                            for kc in range(KC):
                                c.op("pe", lambda e: e.matmul(pg[:, 0:128], lhsT=self.xn[:, kc, tb * 128:(tb + 1) * 128],
                                                              rhs=wb[:, kc, jj * 128:(jj + 1) * 128], start=(kc == 0), stop=(kc == KC - 1)),
                                     reads=[rw, self.r_xn], writes=[rpg], sig=(kc == KC - 1))
                            tbuf, rt, _ = self.tring.next()
                            c.op("act", lambda e: e.activation(out=tbuf[:, 0:128], in_=pg[:, 0:128], func=AF.Copy), reads=[rpg], writes=[rt])
                            r0 = t * TT + tb * 128
                            c.dma("sp", sd, dst[r0:r0 + 128, cc:cc + 128], tbuf[:, 0:128], reads=[rt], writes=[rd])

    def diff_attn(self, layer):
        c, S = self.c, self.S
        H = self.DH
        nb = self.NTOK // S
        lam_init = 0.8 - 0.6 * math.exp(-0.3 * layer)
        ar, tab = self.r_ar, self.r_tab
        sd = c.newsem()
        ds = self.a_ds
        c.dma("sp", sd, ds[:], self.dsmall[layer], reads=[tab], writes=[tab])
        lam = self.a_lam
        tb, rt, _ = self.tring.next()
        c.op("dve", lambda e: e.scalar_tensor_tensor(out=tb[:, 0:64], in0=ds[:, 3:67], scalar=1.0, in1=ds[:, 67:131], op0=ALU.mult, op1=ALU.mult,
                                                     accum_out=lam[:, 0:1]), reads=[tab], writes=[rt, tab])
        c.op("dve", lambda e: e.scalar_tensor_tensor(out=tb[:, 64:128], in0=ds[:, 131:195], scalar=1.0, in1=ds[:, 195:259], op0=ALU.mult, op1=ALU.mult,
                                                     accum_out=lam[:, 1:2]), reads=[tab], writes=[rt, tab])
        c.op("act", lambda e: e.activation(out=lam[:, 2:4], in_=lam[:, 0:2], func=AF.Exp), reads=[tab], writes=[tab])
        c.op("dve", lambda e: e.tensor_tensor(out=lam[:, 4:5], in0=lam[:, 3:4], in1=lam[:, 2:3], op=ALU.subtract), reads=[tab], writes=[tab])
        c.op("dve", lambda e: e.tensor_scalar(out=lam[:, 5:6], in0=lam[:, 4:5], scalar1=-lam_init, scalar2=None, op0=ALU.add), reads=[tab], writes=[tab])
        c.op("dve", lambda e: e.tensor_scalar(out=lam[:, 6:7], in0=ds[:, 0:1], scalar1=64 ** -0.5, scalar2=None, op0=ALU.mult), reads=[tab], writes=[tab])
        c.op("dve", lambda e: e.tensor_scalar(out=lam[:, 7:8], in0=ds[:, 2:3], scalar1=1.0 - lam_init, scalar2=None, op0=ALU.mult), reads=[tab], writes=[tab])
        NS = S // TT
        for b in range(nb):
            for h in range(H):
                slope = 2.0 ** (-8.0 * (h + 1) / H)
                for which, row0, dstt, gcol in ((0, self.o_dq + h * 128, self.a_q, lam[:, 6:7]), (1, self.o_dk + h * 128, self.a_k, ds[:, 1:2])):
                    for sl in range(NS):
                        cs = slice(b * S + sl * TT, b * S + (sl + 1) * TT)
                        hb, rh, sh = self.hring.next()
                        c.dma("sp", sh, hb[:], self.pf(row0, row0 + 128)[:, cs], reads=[self.r_PF], writes=[rh])
                        sq, rsq, _ = self.sqring.next()
                        c.op("act", lambda e: e.activation(out=sq[:], in_=hb[:], func=AF.Square), reads=[rh], writes=[rsq])
                        pb, rpb = self.pbank[7], self.r_pb[7]
                        c.op("pe", lambda e: e.matmul(pb[:], lhsT=self.a_blk[:], rhs=sq[:], start=True, stop=True), reads=[rsq, tab], writes=[rpb])
                        t1, rt1, _ = self.tring.next()
                        c.op("dve", lambda e: e.tensor_scalar(out=t1[:], in0=pb[:], scalar1=1.0 / 64, scalar2=NORM_EPS, op0=ALU.mult, op1=ALU.add),
                             reads=[rpb], writes=[rt1])
                        t2, rt2, _ = self.tring.next()
                        c.op("act", lambda e: e.activation(out=t2[:], in_=t1[:], func=AF.Sqrt), reads=[rt1], writes=[rt2])
                        c.op("dve", lambda e: e.reciprocal(out=t1[:], in_=t2[:]), reads=[rt2], writes=[rt1])
                        c.op("dve", lambda e: e.scalar_tensor_tensor(out=dstt[:, sl * TT:(sl + 1) * TT], in0=hb[:], scalar=gcol, in1=t1[:],
                                                                     op0=ALU.mult, op1=ALU.mult), reads=[rh, rt1, tab], writes=[ar])
                sv = self.hring.sems[0]
                c.dma("pool", self.s_vload, self.a_v[:, :, 0:128],
                      self.VD[b * S:(b + 1) * S, h * 128:(h + 1) * 128].rearrange("(kb p) e -> p kb e", p=128), reads=[self.r_VD], writes=[ar])
                for qc in range(NS):
                    nkb = 4 * (qc + 1)
                    pn = [self.pbank[2], self.pbank[3]]
                    psm = [self.pbank[4], self.pbank[5]]
                    rpn = [self.r_pb[2], self.r_pb[3]]
                    rps = [self.r_pb[4], self.r_pb[5]]
                    for i in range(2):
                        for kb in range(nkb):
                            ps, rps_ = self.pbank[kb % 2], self.r_pb[kb % 2]
                            c.op("pe", lambda e: e.matmul(ps[:], lhsT=self.a_k[i * 64:(i + 1) * 64, kb * 128:(kb + 1) * 128],
                                                          rhs=self.a_q[i * 64:(i + 1) * 64, qc * TT:(qc + 1) * TT], start=True, stop=True),
                                 reads=[ar], writes=[rps_])
                            j = kb - 4 * qc
                            Rm = self.a_R[:, 0, :] if j < 0 else self.a_R[:, 1 + j, :]
                            delta0 = qc * TT - kb * 128
                            t1, rt1, _ = self.tring.next()
                            c.op("dve", lambda e: e.scalar_tensor_tensor(out=t1[:], in0=Rm, scalar=-slope, in1=ps[:], op0=ALU.mult, op1=ALU.add),
                                 reads=[rps_, tab], writes=[rt1])
                            pt, rpt = self.a_pt[self.pti], self.r_pt[self.pti]
                            self.pti = (self.pti + 1) % 3
                            c.op("act", lambda e: e.activation(out=pt[:], in_=t1[:], func=AF.Exp, bias=float(-slope * delta0), scale=1.0),
                                 reads=[rt1], writes=[rpt])
                            c.op("pe", lambda e: e.matmul(pn[i][:], lhsT=self.a_v[:, kb, 0:128], rhs=pt[:], start=(kb == 0), stop=(kb == nkb - 1)),
                                 reads=[rpt, ar], writes=[rpn[i]], sig=False)
                            c.op("pe", lambda e: e.matmul(psm[i][:], lhsT=self.a_onesb[:], rhs=pt[:], start=(kb == 0), stop=(kb == nkb - 1)),
                                 reads=[rpt, tab], writes=[rps[i]])
                    r0, rr0, _ = self.tring.next()
                    c.op("dve", lambda e: e.reciprocal(out=r0[:], in_=psm[0][:]), reads=[rps[0]], writes=[rr0])
                    a0, ra0, _ = self.oring.next()
                    c.op("dve", lambda e: e.tensor_tensor(out=a0[:], in0=pn[0][:], in1=r0[:], op=ALU.mult), reads=[rpn[0], rr0], writes=[ra0])
                    r1, rr1, _ = self.tring.next()
                    c.op("dve", lambda e: e.reciprocal(out=r1[:], in_=psm[1][:]), reads=[rps[1]], writes=[rr1])
                    a1, ra1, _ = self.oring.next()
                    c.op("dve", lambda e: e.tensor_tensor(out=a1[:], in0=pn[1][:], in1=r1[:], op=ALU.mult), reads=[rpn[1], rr1], writes=[ra1])
                    c.op("dve", lambda e: e.scalar_tensor_tensor(out=a0[:], in0=a1[:], scalar=lam[:, 5:6], in1=a0[:], op0=ALU.mult, op1=ALU.add),
                         reads=[ra1, ra0, tab], writes=[ra0])
                    sq, rsq, _ = self.sqring.next()
                    c.op("act", lambda e: e.activation(out=sq[:], in_=a0[:], func=AF.Square), reads=[ra0], writes=[rsq])
                    pb, rpb = self.pbank[7], self.r_pb[7]
                    c.op("pe", lambda e: e.matmul(pb[:], lhsT=self.ones[:], rhs=sq[:], start=True, stop=True), reads=[rsq, self.r_ones], writes=[rpb])
                    t1, rt1, _ = self.tring.next()
                    c.op("dve", lambda e: e.tensor_scalar(out=t1[:], in0=pb[:], scalar1=1.0 / 128, scalar2=NORM_EPS, op0=ALU.mult, op1=ALU.add),
                         reads=[rpb], writes=[rt1])
                    t2, rt2, _ = self.tring.next()
                    c.op("act", lambda e: e.activation(out=t2[:], in_=t1[:], func=AF.Sqrt), reads=[rt1], writes=[rt2])
                    c.op("dve", lambda e: e.reciprocal(out=t1[:], in_=t2[:]), reads=[rt2], writes=[rt1])
                    c.op("dve", lambda e: e.scalar_tensor_tensor(out=a1[:], in0=a0[:], scalar=lam[:, 7:8], in1=t1[:], op0=ALU.mult, op1=ALU.mult),
                         reads=[ra0, rt1, tab], writes=[ra1])
                    row = self.SSMW + self.RW + h * 128
                    c.dma("sp", self.s_YT, self.YT[row:row + 128, b * S + qc * TT:b * S + (qc + 1) * TT], a1[:], reads=[ra1], writes=[self.r_YT])
    def retention(self, layer):
        c, S = self.c, self.S
        H = self.RH
        nb = self.NTOK // S
        ar, tab = self.r_ar, self.r_tab
        NS = S // TT
        NCH = S // 128
        scale = 128 ** -0.5
        for h in range(H):
            lg = math.log(1.0 - 2.0 ** (-5.0 - h))
            c.op("act", lambda e: e.activation(out=self.a_dec[:], in_=self.a_R[:, 1, 0:128], func=AF.Exp, scale=lg, bias=math.log(scale)),
                 reads=[tab, ar], writes=[ar])
            c.op("act", lambda e: e.activation(out=self.a_qdt[:], in_=self.a_io[:], func=AF.Exp, scale=lg, bias=lg), reads=[tab, ar], writes=[ar])
            c.op("act", lambda e: e.activation(out=self.a_kdt[:], in_=self.a_pi[:], func=AF.Exp, scale=-lg, bias=lg * 127 + math.log(scale)),
                 reads=[tab, ar], writes=[ar])
            cdec = math.exp(lg * 128)
            for b in range(nb):
                for sl in range(NS):
                    cs = slice(b * S + sl * TT, b * S + (sl + 1) * TT)
                    hb, rh, sh = self.hring.next()
                    r0 = self.o_rq + h * 128
                    c.dma("sp", sh, hb[:], self.pf(r0, r0 + 128)[:, cs], reads=[self.r_PF], writes=[rh])
                    c.op("act", lambda e: e.activation(out=self.a_q[:, sl * TT:(sl + 1) * TT], in_=hb[:], func=AF.Copy), reads=[rh], writes=[ar])
                    c.op("dve", lambda e: e.tensor_tensor(out=self.a_qd[:, sl * TT:(sl + 1) * TT], in0=hb[:], in1=self.a_qdt[:], op=ALU.mult),
                         reads=[rh, ar], writes=[ar])
                    hb, rh, sh = self.hring.next()
                    r0 = self.o_rk + h * 128
                    c.dma("sp", sh, hb[:], self.pf(r0, r0 + 128)[:, cs], reads=[self.r_PF], writes=[rh])
                    c.op("act", lambda e: e.activation(out=self.a_k[:, sl * TT:(sl + 1) * TT], in_=hb[:], func=AF.Copy), reads=[rh], writes=[ar])
                c.dma("pool", self.s_vload, self.a_v[:, :, :],
                      self.VR[b * S:(b + 1) * S, h * 256:(h + 1) * 256].rearrange("(kb p) e -> p kb e", p=128), reads=[self.r_VR], writes=[ar])
                c.dma("sp", self.s_kload, self.a_kf[:, :, :],
                      self.KR[b * S:(b + 1) * S, h * 128:(h + 1) * 128].rearrange("(kb p) e -> p kb e", p=128), reads=[self.r_KR], writes=[ar])
                c.op("dve", lambda e: e.tensor_scalar(out=self.a_kd[:, :, :], in0=self.a_kf[:, :, :], scalar1=self.a_kdt[:, 0:1], scalar2=None, op0=ALU.mult),
                     reads=[ar], writes=[ar])
                c.op("dve", lambda e: e.memset(self.a_st[:], 0.0), reads=[self.r_st], writes=[self.r_st])
                c.op("dve", lambda e: e.memset(self.a_stb[:], 0.0), reads=[self.r_stb], writes=[self.r_stb])
                for n in range(NCH):
                    ns = slice(n * 128, (n + 1) * 128)
                    sub = n % 4
                    ps, rps_ = self.pbank[n % 2], self.r_pb[n % 2]
                    c.op("pe", lambda e: e.matmul(ps[:, 0:128], lhsT=self.a_k[:, ns], rhs=self.a_q[:, ns], start=True, stop=True), reads=[ar], writes=[rps_])
                    pt, rpt = self.a_pt[self.pti], self.r_pt[self.pti]
                    self.pti = (self.pti + 1) % 3
                    c.op("dve", lambda e: e.tensor_tensor(out=pt[:, 0:128], in0=ps[:, 0:128], in1=self.a_dec[:], op=ALU.mult), reads=[rps_, ar], writes=[rpt])
                    for ec in range(2):
                        po, rpo = self.pbank[2 + ec], self.r_pb[2 + ec]
                        c.op("pe", lambda e: e.matmul(po[:, sub * 128:(sub + 1) * 128], lhsT=self.a_v[:, n, ec * 128:(ec + 1) * 128], rhs=pt[:, 0:128],
                                                      start=True, stop=False), reads=[rpt, ar], writes=[rpo], sig=False)
                        c.op("pe", lambda e: e.matmul(po[:, sub * 128:(sub + 1) * 128], lhsT=self.a_stb[:, ec * 128:(ec + 1) * 128], rhs=self.a_qd[:, ns],
                                                      start=False, stop=True), reads=[self.r_stb, ar], writes=[rpo])
                    pk, rpk = self.pbank[4], self.r_pb[4]
                    c.op("pe", lambda e: e.matmul(pk[:, 0:256], lhsT=self.a_kd[:, n, :], rhs=self.a_v[:, n, :], start=True, stop=True), reads=[ar], writes=[rpk])
                    c.op("dve", lambda e: e.scalar_tensor_tensor(out=self.a_st[:], in0=self.a_st[:], scalar=cdec, in1=pk[:, 0:256], op0=ALU.mult, op1=ALU.add),
                         reads=[rpk, self.r_st], writes=[self.r_st])
                    c.op("act", lambda e: e.activation(out=self.a_stb[:], in_=self.a_st[:], func=AF.Copy), reads=[self.r_st], writes=[self.r_stb])
                    if sub == 3:
                        sl = n // 4
                        cs = slice(b * S + sl * TT, b * S + (sl + 1) * TT)
                        pb, rpb = self.pbank[7], self.r_pb[7]
                        for ec in range(2):
                            po, rpo = self.pbank[2 + ec], self.r_pb[2 + ec]
                            sq, rsq, _ = self.sqring.next()
                            c.op("act", lambda e: e.activation(out=sq[:], in_=po[:], func=AF.Square), reads=[rpo], writes=[rsq])
                            c.op("pe", lambda e: e.matmul(pb[:], lhsT=self.ones[:], rhs=sq[:], start=(ec == 0), stop=(ec == 1)),
                                 reads=[rsq, self.r_ones], writes=[rpb])
                        t1, rt1, _ = self.tring.next()
                        c.op("dve", lambda e: e.tensor_scalar(out=t1[:], in0=pb[:], scalar1=1.0 / 256, scalar2=NORM_EPS, op0=ALU.mult, op1=ALU.add),
                             reads=[rpb], writes=[rt1])
                        t2, rt2, _ = self.tring.next()
                        c.op("act", lambda e: e.activation(out=t2[:], in_=t1[:], func=AF.Sqrt), reads=[rt1], writes=[rt2])
                        c.op("dve", lambda e: e.reciprocal(out=t1[:], in_=t2[:]), reads=[rt2], writes=[rt1])
                        for ec in range(2):
                            po, rpo = self.pbank[2 + ec], self.r_pb[2 + ec]
                            hb, rh, sh = self.hring.next()
                            r0 = self.o_rg + h * 256 + ec * 128
                            c.dma("sp", sh, hb[:], self.pf(r0, r0 + 128)[:, cs], reads=[self.r_PF], writes=[rh])
                            sg, rsg, _ = self.sqring.next()
                            c.op("act", lambda e: e.activation(out=sg[:], in_=hb[:], func=AF.Silu), reads=[rh], writes=[rsg])
                            ob, ro, so = self.oring.next()
                            c.op("dve", lambda e: e.tensor_tensor(out=ob[:], in0=po[:], in1=t1[:], op=ALU.mult), reads=[rpo, rt1], writes=[ro])
                            c.op("dve", lambda e: e.tensor_tensor(out=ob[:], in0=ob[:], in1=sg[:], op=ALU.mult), reads=[ro, rsg], writes=[ro])
                            row = self.SSMW + h * 256 + ec * 128
                            c.dma("sp", self.s_YT, self.YT[row:row + 128, cs], ob[:], reads=[ro], writes=[self.r_YT])

    def merge(self, src, r_src, dst, r_dst, s_dst, layer):
        c, KC, D = self.c, self.KC, self.D
        W = self.mix_w_in[layer]
        gidx = layer * 3 + 1
        widths = [self.SSMW, self.RW, self.DW]
        yoff = [0, self.SSMW, self.SSMW + self.RW]
        WK = self.WTOT // 128
        ybf = self.hid[:, 0:WK, :]
        mrg = self.hid[:, WK:WK + KC, :]
        rh_ = self.r_hid
        for t in range(self.NT):
            ts = slice(t * TT, (t + 1) * TT)
            self.norm_tile(src, r_src, t, gidx)
            c.dma("pool", self.s_yload, ybf, self.YT[:, ts].rearrange("(kc p) n -> p kc n", p=128), reads=[self.r_YT], writes=[rh_])
            for jo0 in range(0, KC, 2):
                for i in range(3):
                    kcn = widths[i] // 128
                    wbr, rwbr = self.load_w(self.w_b[i][layer], 0, kcn, jo0 * 128, 256)
                    wgt, rwgt = self.load_w(W, 0, KC, self.o_g + i * D + jo0 * 128, 256)
                    for jj in range(2):
                        pg, rpg = self.pbank[jj], self.r_pb[jj]
                        for kc in range(KC):
                            c.op("pe", lambda e: e.matmul(pg[:], lhsT=wgt[:, kc, jj * 128:(jj + 1) * 128], rhs=self.xn[:, kc, :],
                                                          start=(kc == 0), stop=(kc == KC - 1)), reads=[rwgt, self.r_xn], writes=[rpg], sig=(kc == KC - 1))
                        pbp, rpbp = self.pbank[2 + jj], self.r_pb[2 + jj]
                        for kc in range(kcn):
                            c.op("pe", lambda e: e.matmul(pbp[:], lhsT=wbr[:, kc, jj * 128:(jj + 1) * 128], rhs=ybf[:, yoff[i] // 128 + kc, :],
                                                          start=(kc == 0), stop=(kc == kcn - 1)), reads=[rwbr, rh_], writes=[rpbp], sig=(kc == kcn - 1))
                        tb, rt, _ = self.tring.next()
                        c.op("act", lambda e: e.activation(out=tb[:], in_=pg[:], func=AF.Sigmoid), reads=[rpg], writes=[rt])
                        acc, racc = self.macc[jj], self.r_macc[jj]
                        if i == 0:
                            c.op("dve", lambda e: e.tensor_tensor(out=acc[:], in0=pbp[:], in1=tb[:], op=ALU.mult), reads=[rpbp, rt], writes=[racc])
                        else:
                            c.op("dve", lambda e: e.tensor_tensor(out=tb[:], in0=pbp[:], in1=tb[:], op=ALU.mult), reads=[rpbp, rt], writes=[rt])
                            if i == 1:
                                c.op("dve", lambda e: e.tensor_tensor(out=acc[:], in0=acc[:], in1=tb[:], op=ALU.add), reads=[racc, rt], writes=[racc])
                            else:
                                c.op("dve", lambda e: e.tensor_tensor(out=mrg[:, jo0 + jj, :], in0=acc[:], in1=tb[:], op=ALU.add),
                                     reads=[racc, rt], writes=[rh_])
            for jo0 in range(0, KC, 2):
                wo, rwo = self.load_w(self.w_o[layer], 0, KC, jo0 * 128, 256)
                for jj in range(2):
                    po, rpo = self.pbank[4 + jj], self.r_pb[4 + jj]
                    for kc in range(KC):
                        c.op("pe", lambda e: e.matmul(po[:], lhsT=wo[:, kc, jj * 128:(jj + 1) * 128], rhs=mrg[:, kc, :],
                                                      start=(kc == 0), stop=(kc == KC - 1)), reads=[rwo, rh_], writes=[rpo], sig=(kc == KC - 1))
                    jo = jo0 + jj
                    hb, rh, sh = self.hring.next()
                    c.dma("sp", sh, hb[:], src[jo * 128:(jo + 1) * 128, ts], reads=[r_src], writes=[rh])
                    ob, ro, so = self.oring.next()
                    c.op("dve", lambda e: e.tensor_tensor(out=ob[:], in0=po[:], in1=hb[:], op=ALU.add), reads=[rpo, rh], writes=[ro])
                    c.dma("sp", s_dst, dst[jo * 128:(jo + 1) * 128, ts], ob[:], reads=[ro], writes=[r_dst])

    def setup_s5(self):
        c, nc = self.c, self.nc
        DEPTH = self.DEPTH
        self.G = self.SSMW // 16
        self.NP = NP = self.G // 2
        self.CB = CB = self.SSMW // 128
        self.TS = TS = 128
        self.s5p = nc.dram_tensor("s5p", [DEPTH, 128, 3 * NP], F32, kind="ExternalInput").ap()
        self.s5bt = nc.dram_tensor("s5bt", [DEPTH, 2, 128, NP, 128], F32, kind="ExternalInput").ap()
        self.s5ct = nc.dram_tensor("s5ct", [DEPTH, 2, 128, NP, 128], F32, kind="ExternalInput").ap()
        self.s5d = nc.dram_tensor("s5d", [DEPTH, 128, 2 * CB], F32, kind="ExternalInput").ap()
        self.glu_w = nc.dram_tensor("ssm_glu_w", [DEPTH, self.SSMW // 256, 128, self.SSMW // 128, 256], F32, kind="ExternalInput").ap()
        off = [self.hid_off]

        def at(name, shape, dt, esz):
            n = 1
            for d in shape[1:]:
                n *= d
            t = nc.alloc_sbuf_tensor_at(name, shape, dt, offset=off[0])
            off[0] += ((n * esz + 31) // 32) * 32
            return t
        I32 = mybir.dt.int32
        self.s_cos = at("s_cos", [128, NP, TS], F32, 4)
        self.s_sin = at("s_sin", [128, NP, TS], F32, 4)
        self.s_bt = at("s_bt", [128, 2, NP, 128], BF16, 2)
        self.s_cp = at("s_cp", [128, 2, NP, 128], BF16, 2)
        self.s_w = [self.sqring.bufs[0], self.sqring.bufs[1], self.macc[0], self.macc[1]] + [at("s_w%d" % i, [128, TT], F32, 4) for i in range(2)]
        self.s_wi = [at("s_wi%d" % i, [128, TT], I32, 4) for i in range(1)]
        self.s_x = [at("s_x%d" % i, [128, TT], BF16, 2) for i in range(2)]
        self.s_prm = at("s_prm", [128, 3 * NP], F32, 4)
        self.s_sm = at("s_sm", [128, 12, NP], F32, 4)
        self.s_z0 = at("s_z0", [128, 2, NP], F32, 4)
        self.s_dd = at("s_dd", [128, 2 * CB], F32, 4)
        self.s_ri = at("s_ri", [128, TS], F32, 4)
        self.s_tz = at("s_tz", [128, 4], F32, 4)
        assert off[0] - self.hid_off <= self.HC * TT * 2, (off[0] - self.hid_off, self.HC * TT * 2)
        off = [self.xn_off]
        assert 2 * NP * 128 * 4 <= self.KC * TT * 2
        self.s_ctf = at("s_ctf", [128, 2, NP, 128], F32, 4)
        off = [self.xn_off]
        self.s_ge = at("s_ge", [128, CB, TT], F32, 4)
        self.s_geb = at("s_geb", [128, CB, TT], BF16, 2)
        self.s_ub = at("s_ub", [128, CB, TT], BF16, 2)
        assert off[0] - self.xn_off <= self.KC * TT * 2, (off[0] - self.xn_off, self.KC * TT * 2)
        self.r_s5 = Res("s5arena")
        self.r_s5x = Res("s5xn")
        self.r_sx = Res("s5x")
        self.s_s5 = c.newsem()

    def _frac_sin(self, dst, ang, w1, w2, i32, shift):
        c, r = self.c, self.r_s5
        if shift != 0.0:
            c.op("dve", lambda e: e.tensor_scalar(out=w1, in0=ang, scalar1=shift, scalar2=None, op0=ALU.add), reads=[r], writes=[r])
            src = w1
        else:
            src = ang
        c.op("dve", lambda e: e.tensor_copy(out=i32, in_=src), reads=[r], writes=[r])
        c.op("dve", lambda e: e.tensor_copy(out=w2, in_=i32), reads=[r], writes=[r])
        c.op("dve", lambda e: e.tensor_tensor(out=w1, in0=src, in1=w2, op=ALU.subtract), reads=[r], writes=[r])
        c.op("dve", lambda e: e.tensor_scalar(out=w2, in0=w1, scalar1=0.5, scalar2=None, op0=ALU.is_gt), reads=[r], writes=[r])
        c.op("dve", lambda e: e.tensor_tensor(out=w1, in0=w1, in1=w2, op=ALU.subtract), reads=[r], writes=[r])
        c.op("dve", lambda e: e.tensor_scalar(out=w2, in0=w1, scalar1=-0.5, scalar2=None, op0=ALU.is_lt), reads=[r], writes=[r])
        c.op("dve", lambda e: e.tensor_tensor(out=w1, in0=w1, in1=w2, op=ALU.add), reads=[r], writes=[r])
        c.op("act", lambda e: e.activation(out=dst, in_=w1, func=AF.Sin, scale=2.0 * math.pi), reads=[r], writes=[r])

    def s5_prep(self, layer):
        c, r, NP, TS = self.c, self.r_s5, self.NP, self.TS
        sm = self.s_sm
        prm = self.s_prm
        c.dma("sp", self.s_s5, prm[:], self.s5p[layer], reads=[r], writes=[r])
        c.dma("sp", self.s_s5, self.s_dd[:], self.s5d[layer], reads=[r], writes=[r])
        c.dma("pool", self.s_s5, self.s_bt[:], self.s5bt[layer].rearrange("t p n m -> p t n m"), reads=[r], writes=[r])
        c.dma("sp", self.s_s5, self.s_ctf[:], self.s5ct[layer].rearrange("t p n m -> p t n m"), reads=[self.r_s5x, self.r_xn], writes=[self.r_s5x])
        lr, li, ldt = prm[:, 0:NP], prm[:, NP:2 * NP], prm[:, 2 * NP:3 * NP]
        dt, th, mag, f0 = sm[:, 0, :], sm[:, 1, :], sm[:, 2, :], sm[:, 3, :]
        nr, ni, rden, cre, cim, ncim, t1, t2 = (sm[:, k, :] for k in range(4, 12))
        c.op("act", lambda e: e.activation(out=dt, in_=ldt, func=AF.Exp), reads=[r], writes=[r])
        c.op("dve", lambda e: e.tensor_tensor(out=th, in0=li, in1=dt, op=ALU.mult), reads=[r], writes=[r])
        c.op("dve", lambda e: e.tensor_tensor(out=t1, in0=lr, in1=dt, op=ALU.mult), reads=[r], writes=[r])
        c.op("act", lambda e: e.activation(out=mag, in_=t1, func=AF.Exp), reads=[r], writes=[r])
        c.op("dve", lambda e: e.tensor_scalar(out=t1, in0=th, scalar1=1.0 / (2.0 * math.pi), scalar2=None, op0=ALU.mult), reads=[r], writes=[r])
        i32s = self.s_wi[0]
        c.op("dve", lambda e: e.tensor_copy(out=i32s[:, 0:NP], in_=t1), reads=[r], writes=[r])
        c.op("dve", lambda e: e.tensor_copy(out=t2, in_=i32s[:, 0:NP]), reads=[r], writes=[r])
        c.op("dve", lambda e: e.tensor_tensor(out=f0, in0=t1, in1=t2, op=ALU.subtract), reads=[r], writes=[r])
        c.op("pool", lambda e: e.iota(i32s[:, 0:TS], pattern=[[1, TS]], base=1, channel_multiplier=0), reads=[r], writes=[r])
        c.op("dve", lambda e: e.tensor_copy(out=self.s_ri[:], in_=i32s[:, 0:TS]), reads=[r], writes=[r])
        ang, w1, w2 = self.s_w[0], self.s_w[1], self.s_w[2]
        for g0 in range(0, NP, 4):
            for k in range(4):
                gp = g0 + k
                c.op("dve", lambda e: e.tensor_scalar(out=ang[:, k * TS:(k + 1) * TS], in0=self.s_ri[:], scalar1=f0[:, gp:gp + 1], scalar2=None, op0=ALU.mult),
                     reads=[r], writes=[r])
            self._frac_sin(self.s_sin[:, g0:g0 + 4, :].rearrange("p n t -> p (n t)"), ang[:], w1[:], w2[:], i32s[:], 0.0)
            self._frac_sin(self.s_cos[:, g0:g0 + 4, :].rearrange("p n t -> p (n t)"), ang[:], w1[:], w2[:], i32s[:], 0.25)
        c.op("dve", lambda e: e.tensor_tensor(out=nr, in0=mag, in1=self.s_cos[:, :, 0], op=ALU.mult), reads=[r], writes=[r])
        c.op("dve", lambda e: e.tensor_scalar(out=nr, in0=nr, scalar1=-1.0, scalar2=None, op0=ALU.add), reads=[r], writes=[r])
        c.op("dve", lambda e: e.tensor_tensor(out=ni, in0=mag, in1=self.s_sin[:, :, 0], op=ALU.mult), reads=[r], writes=[r])
        c.op("dve", lambda e: e.tensor_tensor(out=t1, in0=lr, in1=lr, op=ALU.mult), reads=[r], writes=[r])
        c.op("dve", lambda e: e.tensor_tensor(out=t2, in0=li, in1=li, op=ALU.mult), reads=[r], writes=[r])
        c.op("dve", lambda e: e.tensor_tensor(out=t1, in0=t1, in1=t2, op=ALU.add), reads=[r], writes=[r])
        c.op("dve", lambda e: e.reciprocal(out=rden, in_=t1), reads=[r], writes=[r])
        c.op("dve", lambda e: e.tensor_tensor(out=t1, in0=nr, in1=lr, op=ALU.mult), reads=[r], writes=[r])
        c.op("dve", lambda e: e.tensor_tensor(out=t2, in0=ni, in1=li, op=ALU.mult), reads=[r], writes=[r])
        c.op("dve", lambda e: e.tensor_tensor(out=t1, in0=t1, in1=t2, op=ALU.add), reads=[r], writes=[r])
        c.op("dve", lambda e: e.tensor_tensor(out=cre, in0=t1, in1=rden, op=ALU.mult), reads=[r], writes=[r])
        c.op("dve", lambda e: e.tensor_tensor(out=t1, in0=ni, in1=lr, op=ALU.mult), reads=[r], writes=[r])
        c.op("dve", lambda e: e.tensor_tensor(out=t2, in0=nr, in1=li, op=ALU.mult), reads=[r], writes=[r])
        c.op("dve", lambda e: e.tensor_tensor(out=t1, in0=t1, in1=t2, op=ALU.subtract), reads=[r], writes=[r])
        c.op("dve", lambda e: e.tensor_tensor(out=cim, in0=t1, in1=rden, op=ALU.mult), reads=[r], writes=[r])
        c.op("dve", lambda e: e.tensor_scalar(out=ncim, in0=cim, scalar1=-1.0, scalar2=None, op0=ALU.mult), reads=[r], writes=[r])
        tw = self.s_w[0]
        for gp in range(NP):
            c.op("dve", lambda e: e.tensor_scalar(out=tw[:, 0:128], in0=self.s_ctf[:, 1, gp, :], scalar1=cim[:, gp:gp + 1], scalar2=None, op0=ALU.mult),
                 reads=[r, self.r_s5x], writes=[r])
            c.op("dve", lambda e: e.scalar_tensor_tensor(out=self.s_cp[:, 0, gp, :], in0=self.s_ctf[:, 0, gp, :], scalar=cre[:, gp:gp + 1], in1=tw[:, 0:128],
                                                         op0=ALU.mult, op1=ALU.subtract), reads=[r, self.r_s5x], writes=[r])
            c.op("dve", lambda e: e.tensor_scalar(out=tw[:, 128:256], in0=self.s_ctf[:, 1, gp, :], scalar1=cre[:, gp:gp + 1], scalar2=None, op0=ALU.mult),
                 reads=[r, self.r_s5x], writes=[r])
            c.op("dve", lambda e: e.scalar_tensor_tensor(out=self.s_cp[:, 1, gp, :], in0=self.s_ctf[:, 0, gp, :], scalar=ncim[:, gp:gp + 1], in1=tw[:, 128:256],
                                                         op0=ALU.mult, op1=ALU.subtract), reads=[r, self.r_s5x], writes=[r])

    def s5(self, layer):
        c, r, rx, NP, TS, CB, S = self.c, self.r_s5, self.r_s5x, self.NP, self.TS, self.CB, self.S
        self.s5_prep(layer)
        nb = self.NTOK // S
        NS = S // TT
        NCK = TT // TS
        mag = self.s_sm[:, 2, :]
        cosb = lambda gp: self.s_cos[:, gp:gp + 1, :].broadcast_to([128, NCK, TS])
        sinb = lambda gp: self.s_sin[:, gp:gp + 1, :].broadcast_to([128, NCK, TS])
        v3 = lambda t: t[:].rearrange("p (n t) -> p n t", t=TS)
        w = self.s_w
        for b in range(nb):
            c.op("dve", lambda e: e.memset(self.s_z0[:], 0.0), reads=[r], writes=[r])
            for sl in range(NS):
                cs = slice(b * S + sl * TT, b * S + (sl + 1) * TT)
                c.dma("pool", self.s_s5, self.s_ub[:], self.pf(self.o_u, self.o_u + self.SSMW)[:, cs].rearrange("(cb p) n -> p cb n", p=128),
                      reads=[self.r_PF, rx], writes=[rx])
                for gp in range(NP):
                    cb = gp // 4
                    p0, rp0 = self.pbank[0], self.r_pb[0]
                    p1, rp1 = self.pbank[1], self.r_pb[1]
                    c.op("pe", lambda e: e.matmul(p0[:], lhsT=self.s_bt[:, 0, gp, :], rhs=self.s_ub[:, cb, :], start=True, stop=True), reads=[r, rx], writes=[rp0])
                    c.op("pe", lambda e: e.matmul(p1[:], lhsT=self.s_bt[:, 1, gp, :], rhs=self.s_ub[:, cb, :], start=True, stop=True), reads=[r, rx], writes=[rp1])
                    c.op("dve", lambda e: e.tensor_tensor(out=v3(w[0]), in0=v3(p0), in1=cosb(gp), op=ALU.mult), reads=[rp0, r], writes=[r])
                    c.op("dve", lambda e: e.tensor_tensor(out=v3(w[1]), in0=v3(p1), in1=sinb(gp), op=ALU.mult), reads=[rp1, r], writes=[r])
                    c.op("dve", lambda e: e.tensor_tensor(out=v3(w[2]), in0=v3(p1), in1=cosb(gp), op=ALU.mult), reads=[rp1, r], writes=[r])
                    c.op("dve", lambda e: e.tensor_tensor(out=v3(w[3]), in0=v3(p0), in1=sinb(gp), op=ALU.mult), reads=[rp0, r], writes=[r])
                    c.op("pool", lambda e: e.tensor_tensor(out=w[0][:], in0=w[0][:], in1=w[1][:], op=ALU.add), reads=[r], writes=[r])
                    c.op("pool", lambda e: e.tensor_tensor(out=w[2][:], in0=w[2][:], in1=w[3][:], op=ALU.subtract), reads=[r], writes=[r])
                    zr, zi = w[4], w[5]
                    ct = self.s_cos[:, gp, TS - 1:TS]
                    st = self.s_sin[:, gp, TS - 1:TS]
                    for n in range(NCK):
                        ns = slice(n * TS, (n + 1) * TS)
                        magb = mag[:, gp:gp + 1].broadcast_to([128, TS])
                        c.op("dve", lambda e: e.tensor_tensor_scan(out=zr[:, ns], data0=magb, data1=w[0][:, ns], initial=self.s_z0[:, 0, gp:gp + 1],
                                                                   op0=ALU.mult, op1=ALU.add), reads=[r], writes=[r])
                        c.op("dve", lambda e: e.tensor_tensor_scan(out=zi[:, ns], data0=magb, data1=w[2][:, ns], initial=self.s_z0[:, 1, gp:gp + 1],
                                                                   op0=ALU.mult, op1=ALU.add), reads=[r], writes=[r])
                        e0 = (n + 1) * TS - 1
                        tz = self.s_tz
                        c.op("dve", lambda e: e.tensor_scalar(out=tz[:, 0:1], in0=zi[:, e0:e0 + 1], scalar1=st, scalar2=None, op0=ALU.mult), reads=[r], writes=[r])
                        c.op("dve", lambda e: e.tensor_scalar(out=tz[:, 1:2], in0=zr[:, e0:e0 + 1], scalar1=st, scalar2=None, op0=ALU.mult), reads=[r], writes=[r])
                        c.op("dve", lambda e: e.scalar_tensor_tensor(out=self.s_z0[:, 0, gp:gp + 1], in0=zr[:, e0:e0 + 1], scalar=ct, in1=tz[:, 0:1],
                                                                     op0=ALU.mult, op1=ALU.subtract), reads=[r], writes=[r])
                        c.op("dve", lambda e: e.scalar_tensor_tensor(out=self.s_z0[:, 1, gp:gp + 1], in0=zi[:, e0:e0 + 1], scalar=ct, in1=tz[:, 1:2],
                                                                     op0=ALU.mult, op1=ALU.add), reads=[r], writes=[r])
                    c.op("pool", lambda e: e.tensor_tensor(out=v3(w[0]), in0=v3(zr), in1=cosb(gp), op=ALU.mult), reads=[r], writes=[r])
                    c.op("pool", lambda e: e.tensor_tensor(out=v3(w[1]), in0=v3(zi), in1=sinb(gp), op=ALU.mult), reads=[r], writes=[r])
                    c.op("pool", lambda e: e.tensor_tensor(out=v3(w[2]), in0=v3(zr), in1=sinb(gp), op=ALU.mult), reads=[r], writes=[r])
                    c.op("pool", lambda e: e.tensor_tensor(out=v3(w[3]), in0=v3(zi), in1=cosb(gp), op=ALU.mult), reads=[r], writes=[r])
                    rxs = self.r_sx
                    c.op("pool", lambda e: e.tensor_tensor(out=self.s_x[0][:], in0=w[0][:], in1=w[1][:], op=ALU.subtract), reads=[r, rxs], writes=[rxs])
                    c.op("pool", lambda e: e.tensor_tensor(out=self.s_x[1][:], in0=w[2][:], in1=w[3][:], op=ALU.add), reads=[r, rxs], writes=[rxs])
                    py, rpy = self.pbank[2], self.r_pb[2]
                    k4 = gp % 4
                    c.op("pe", lambda e: e.matmul(py[:], lhsT=self.s_cp[:, 0, gp, :], rhs=self.s_x[0][:], start=(k4 == 0), stop=False), reads=[r, rxs], writes=[rpy], sig=False)
                    c.op("pe", lambda e: e.matmul(py[:], lhsT=self.s_cp[:, 1, gp, :], rhs=self.s_x[1][:], start=False, stop=(k4 == 3)), reads=[r, rxs], writes=[rpy])
                    if k4 == 3:
                        hb, rh, sh = self.hring.next()
                        r0 = self.o_u + cb * 128
                        c.dma("sp", sh, hb[:], self.pf(r0, r0 + 128)[:, cs], reads=[self.r_PF], writes=[rh])
                        yv, ryv, _ = self.oring.next()
                        c.op("dve", lambda e: e.scalar_tensor_tensor(out=yv[:], in0=hb[:], scalar=self.s_dd[:, cb:cb + 1], in1=py[:], op0=ALU.mult, op1=ALU.add),
                             reads=[rh, rpy, r], writes=[ryv])
                        t1, rt1, _ = self.tring.next()
                        c.op("dve", lambda e: e.tensor_tensor(out=t1[:], in0=yv[:], in1=yv[:], op=ALU.mult), reads=[ryv], writes=[rt1])
                        c.op("dve", lambda e: e.tensor_scalar(out=t1[:], in0=t1[:], scalar1=0.044715, scalar2=1.0, op0=ALU.mult, op1=ALU.add), reads=[rt1], writes=[rt1])
                        c.op("dve", lambda e: e.tensor_tensor(out=t1[:], in0=t1[:], in1=yv[:], op=ALU.mult), reads=[rt1, ryv], writes=[rt1])
                        t2, rt2, _ = self.tring.next()
                        c.op("act", lambda e: e.activation(out=t2[:], in_=t1[:], func=AF.Sigmoid, scale=2.0 * math.sqrt(2.0 / math.pi)), reads=[rt1], writes=[rt2])
                        c.op("dve", lambda e: e.tensor_tensor(out=self.s_ge[:, cb, :], in0=yv[:], in1=t2[:], op=ALU.mult), reads=[ryv, rt2, rx], writes=[rx])
                        c.op("act", lambda e: e.activation(out=self.s_geb[:, cb, :], in_=self.s_ge[:, cb, :], func=AF.Copy), reads=[rx], writes=[rx])
                for ob0 in range(0, CB, 2):
                    wg, rwg = self.load_w(self.glu_w[layer], 0, CB, ob0 * 128, 256)
                    for jj in range(2):
                        ob = ob0 + jj
                        pg, rpg = self.pbank[3 + jj], self.r_pb[3 + jj]
                        for kc in range(CB):
                            c.op("pe", lambda e: e.matmul(pg[:], lhsT=wg[:, kc, jj * 128:(jj + 1) * 128], rhs=self.s_geb[:, kc, :], start=(kc == 0), stop=(kc == CB - 1)),
                                 reads=[rwg, rx], writes=[rpg], sig=(kc == CB - 1))
                        t1, rt1, _ = self.tring.next()
                        c.op("act", lambda e: e.activation(out=t1[:], in_=pg[:], func=AF.Sigmoid, bias=self.s_dd[:, CB + ob:CB + ob + 1], scale=1.0),
                             reads=[rpg, r], writes=[rt1])
                        yo, ryo, _ = self.oring.next()
                        c.op("dve", lambda e: e.tensor_tensor(out=yo[:], in0=self.s_ge[:, ob, :], in1=t1[:], op=ALU.mult), reads=[rx, rt1], writes=[ryo])
                        c.dma("sp", self.s_YT, self.YT[ob * 128:(ob + 1) * 128, cs], yo[:], reads=[ryo], writes=[self.r_YT])


def host_layout_s5(inputs, DEPTH, D):
    f = lambda k: np.asarray(inputs[k], dtype=np.float32)
    G = (D // 4) // 16
    NP = G // 2
    CB = (D // 4) // 128
    P = 64

    def pl(a):
        return a.reshape(DEPTH, NP, 2, P).transpose(0, 2, 3, 1).reshape(DEPTH, 128, NP)
    ldt = np.broadcast_to(f("ssm_log_dt")[:, :, None], (DEPTH, G, P))
    s5p = np.concatenate([pl(f("ssm_lambda_re")), pl(f("ssm_lambda_im")), pl(ldt)], axis=2)
    bt = np.zeros((DEPTH, 2, 128, NP, 128), np.float32)
    ct = np.zeros((DEPTH, 2, 128, NP, 128), np.float32)
    for t, (kb, kc) in enumerate((("ssm_b_re", "ssm_c_re"), ("ssm_b_im", "ssm_c_im"))):
        b = f(kb)
        cc = f(kc)
        for g in range(G):
            gp, gl = g // 2, g % 2
            r0 = (g % 8) * 16
            bt[:, t, r0:r0 + 16, gp, gl * P:(gl + 1) * P] = b[:, g].transpose(0, 2, 1)
            ct[:, t, gl * P:(gl + 1) * P, gp, r0:r0 + 16] = cc[:, g].transpose(0, 2, 1)
    dd = np.concatenate([f("ssm_d").reshape(DEPTH, CB, 128).transpose(0, 2, 1), f("ssm_glu_b").reshape(DEPTH, CB, 128).transpose(0, 2, 1)], axis=2)
    return {"s5p": np.ascontiguousarray(s5p), "s5bt": bt, "s5ct": ct, "s5d": np.ascontiguousarray(dd), "ssm_glu_w": tile_w(f("ssm_glu_w"))}


def build(D, B, S, DEPTH, D_FF, debug=False, nseq=1):
    m = Model(D, B, S, DEPTH, D_FF, debug=debug)
    m.setup_mixer()
    m.setup_s5()
    c = m.c
    bufs = [(m.hA, m.r_hA, m.s_hA), (m.hB, m.r_hB, m.s_hB)]
    cur, r_cur = m.xT, m.r_xT
    bi = 0
    for layer in range(DEPTH):
        dst, r_dst, s_dst = bufs[bi]; bi ^= 1
        m.ffn(cur, r_cur, dst, r_dst, s_dst, layer, 0)
        cur, r_cur = dst, r_dst
        m.inproj(cur, r_cur, layer)
        c.barrier()
        m.mixer_consts()
        m.diff_attn(layer)
        m.retention(layer)
        c.barrier()
        m.s5(layer)
        c.barrier()
        dst, r_dst, s_dst = bufs[bi]; bi ^= 1
        m.merge(cur, r_cur, dst, r_dst, s_dst, layer)
        cur, r_cur = dst, r_dst
        if layer == DEPTH - 1:
            dst, r_dst, s_dst = m.outT, m.r_out, m.s_out
        else:
            dst, r_dst, s_dst = bufs[bi]; bi ^= 1
        m.ffn(cur, r_cur, dst, r_dst, s_dst, layer, 1)
        cur, r_cur = dst, r_dst
    outs = [m.r_out]
    if debug:
        outs.append(m.r_YT)
    c.finish(outs)
    return m


N_CORES = 2


def tile_w(w):
    lead = w.shape[:-2]
    K, N = w.shape[-2:]
    nl = len(lead)
    v = w.reshape(*lead, K // 128, 128, N // 256, 256)
    return np.ascontiguousarray(v.transpose(*range(nl), nl + 2, nl + 1, nl, nl + 3))


def host_layout(inputs, DEPTH, D):
    f = lambda k: np.asarray(inputs[k], dtype=np.float32)
    KC = D // 128
    out = {}
    out["normw"] = np.ascontiguousarray(f("norm_w").reshape(DEPTH * 3, KC, 128).transpose(0, 2, 1))
    ds = np.zeros((DEPTH, 128, 3 + 4 * 64), np.float32)
    ds[:, :, 0] = np.tile(f("diff_q_gain"), (1, 2))
    ds[:, :, 1] = np.tile(f("diff_k_gain"), (1, 2))
    ds[:, :, 2] = f("diff_sub_gain")
    for j, k in enumerate(("diff_lambda_q1", "diff_lambda_k1", "diff_lambda_q2", "diff_lambda_k2")):
        ds[:, :, 3 + 64 * j:3 + 64 * (j + 1)] = f(k)[:, None, :]
    out["dsmall"] = ds
    for k in ("ffn_w_in", "ffn_w_out", "mix_w_in", "w_branch_ssm", "w_branch_ret", "w_branch_diff", "w_out"):
        out[k] = tile_w(f(k))
    out.update(host_layout_s5(inputs, DEPTH, D))
    return out


def kernel(_debug=False, **inputs):
    x = np.asarray(inputs["x"], dtype=np.float32)
    B, S, D = x.shape
    DEPTH = np.asarray(inputs["norm_w"]).shape[0]
    D_FF = np.asarray(inputs["ffn_w_out"]).shape[2]
    ncores = N_CORES if B % N_CORES == 0 else 1
    nseq = B // ncores
    m = build(D, nseq, S, DEPTH, D_FF, debug=_debug, nseq=nseq)
    shared = host_layout(inputs, DEPTH, D)
    xT = x.reshape(B * S, D).T
    per = nseq * S
    in_maps = []
    for i in range(ncores):
        d = dict(shared)
        d["xT"] = np.ascontiguousarray(xT[:, i * per:(i + 1) * per])
        in_maps.append(d)
    res = run_bass_kernel_spmd(m.nc, in_maps, core_ids=list(range(ncores)))
    outT = np.concatenate([r["outT"] for r in res.results], axis=1)
    out = np.ascontiguousarray(outT.T).reshape(B, S, D).astype(np.float32)
    if _debug:
        return out, np.concatenate([r["YT"] for r in res.results], axis=1)
    return out
```

```python
import math
import numpy as np
import concourse.bass as bass
import concourse.mybir as mybir
from concourse.bass_utils import run_bass_kernel_spmd

F32 = mybir.dt.float32
BF16 = mybir.dt.bfloat16
AF = mybir.ActivationFunctionType
ALU = mybir.AluOpType

NORM_EPS = 1e-6
TT = 512


class Res:
    __slots__ = ("name", "last_w", "readers")

    def __init__(self, name):
        self.name = name
        self.last_w = None
        self.readers = {}


class Ctx:
    def __init__(self):
        nc = self.nc = bass.Bass("TRN2", target_bir_lowering=False)
        self.eng = {"pe": nc.tensor, "act": nc.scalar, "dve": nc.vector, "pool": nc.gpsimd, "sp": nc.sync}
        self.semobj = {}
        self.cnt = {}
        for e in self.eng:
            self.semobj[e] = nc.alloc_semaphore("s_" + e)
            self.cnt[e] = 0
        self.known = {e: {} for e in self.eng}
        self.nsem = 0

    def newsem(self):
        k = "d%d" % self.nsem
        self.nsem += 1
        self.semobj[k] = self.nc.alloc_semaphore(k)
        self.cnt[k] = 0
        return k

    def sb(self, name, shape, dt):
        return self.nc.alloc_sbuf_tensor(name, shape, dt)

    def ps(self, name, shape, dt=F32):
        return self.nc.alloc_psum_tensor(name, shape, dt)

    def _wait(self, e, dep):
        if dep is None:
            return
        k, v = dep
        if self.known[e].get(k, 0) >= v:
            return
        if k == e and v > self.cnt[e]:
            return
        self.eng[e].wait_ge(self.semobj[k], v)
        self.known[e][k] = v

    def _deps(self, e, reads, writes):
        for r in reads:
            self._wait(e, r.last_w)
        for w in writes:
            self._wait(e, w.last_w)
            for k, v in w.readers.items():
                self._wait(e, (k, v))

    def op(self, e, fn, reads=(), writes=(), sig=True):
        self._deps(e, reads, writes)
        inst = fn(self.eng[e])
        if sig:
            self.cnt[e] += 1
            inst.then_inc(self.semobj[e], 1)
            c = self.cnt[e]
        else:
            c = self.cnt[e] + 1
        for r in reads:
            r.readers[e] = c
        for w in writes:
            w.last_w = (e, c)
            w.readers = {}
        return inst

    def dma(self, q, sem, out, in_, reads=(), writes=(), **kw):
        self._deps(q, reads, writes)
        inst = self.eng[q].dma_start(out=out, in_=in_, **kw)
        self.cnt[sem] += 16
        inst.then_inc(self.semobj[sem], 16)
        c = self.cnt[sem]
        for r in reads:
            r.readers[sem] = c
        for w in writes:
            w.last_w = (sem, c)
            w.readers = {}
        return inst

    def barrier(self):
        snap = dict(self.cnt)
        for e in self.eng:
            for k, v in snap.items():
                if k != e and v > 0:
                    self._wait(e, (k, v))

    def finish(self, outs):
        for r in outs:
            self._wait("sp", r.last_w)


class Ring:
    def __init__(self, c, name, n, shape, dt):
        self.bufs = [c.sb("%s%d" % (name, i), shape, dt) for i in range(n)]
        self.res = [Res("%s%d" % (name, i)) for i in range(n)]
        self.sems = [c.newsem() for _ in range(n)]
        self.i = 0
        self.n = n

    def next(self):
        i = self.i
        self.i = (i + 1) % self.n
        return self.bufs[i], self.res[i], self.sems[i]


class Model:
    def __init__(self, D, B, S, DEPTH, D_FF, debug=False):
        self.debug = debug
        self.D, self.B, self.S, self.DEPTH, self.D_FF = D, B, S, DEPTH, D_FF
        self.KC = D // 128
        self.HC = D_FF // 128
        self.NTOK = B * S
        self.NT = self.NTOK // TT
        c = self.c = Ctx()
        nc = self.nc = c.nc
        KC, HC = self.KC, self.HC
        self.xT = nc.dram_tensor("xT", [D, self.NTOK], F32, kind="ExternalInput").ap()
        self.normw = nc.dram_tensor("normw", [DEPTH * 3, 128, KC], F32, kind="ExternalInput").ap()
        self.ffn_w_in = nc.dram_tensor("ffn_w_in", [DEPTH, 2, 2 * D_FF // 256, 128, KC, 256], F32, kind="ExternalInput").ap()
        self.ffn_w_out = nc.dram_tensor("ffn_w_out", [DEPTH, 2, D // 256, 128, HC, 256], F32, kind="ExternalInput").ap()
        self.outT = nc.dram_tensor("outT", [D, self.NTOK], F32, kind="ExternalOutput").ap()
        self.r_out = Res("outT")
        self.s_out = c.newsem()
        self.hA = nc.dram_tensor("hA", [D, self.NTOK], F32, kind="Internal").ap()
        self.hB = nc.dram_tensor("hB", [D, self.NTOK], F32, kind="Internal").ap()
        self.r_hA, self.r_hB = Res("hA"), Res("hB")
        self.s_hA, self.s_hB = c.newsem(), c.newsem()
        self.r_xT = Res("xT")
        self.gains = c.sb("gains", [128, DEPTH * 3, KC], F32)
        self.r_gains = Res("gains")
        self.ones = c.sb("ones", [128, 128], F32)
        self.r_ones = Res("ones")
        self.xn_off = ((nc.sbuf_base + 31) // 32) * 32
        self.xn = c.sb("xn", [128, KC, TT], BF16)
        self.r_xn = Res("xn")
        self.hid_off = ((nc.sbuf_base + 31) // 32) * 32
        self.hid = c.sb("hid", [128, HC, TT], BF16)
        self.r_hid = Res("hid")
        self.rstd = c.sb("rstd", [128, TT], F32)
        self.r_rstd = Res("rstd")
        self.wring = Ring(c, "w", 3, [128, 32, 256], BF16)
        self.hring = Ring(c, "hc", 3, [128, TT], F32)
        self.sqring = Ring(c, "sq", 2, [128, TT], F32)
        self.tring = Ring(c, "tmp", 3, [128, TT], F32)
        self.oring = Ring(c, "o", 2, [128, TT], F32)
        self.macc = [c.sb("macc%d" % i, [128, TT], F32) for i in range(2)]
        self.r_macc = [Res("macc%d" % i) for i in range(2)]
        self.s_vload, self.s_kload, self.s_yload = c.newsem(), c.newsem(), c.newsem()
        self.pbank = [c.ps("pb%d" % i, [128, TT]) for i in range(8)]
        self.r_pb = [Res("pb%d" % i) for i in range(8)]
        c.op("dve", lambda e: e.memset(self.ones[:], 1.0), writes=[self.r_ones])
        sg = c.newsem()
        c.dma("sp", sg, self.gains[:], self.normw.rearrange("n p k -> p n k"), writes=[self.r_gains])

    def norm_tile(self, src, r_src, t, gidx):
        c, KC, D = self.c, self.KC, self.D
        ts = slice(t * TT, (t + 1) * TT)
        pb, rpb = self.pbank[7], self.r_pb[7]
        for kc in range(KC):
            hb, rh, sh = self.hring.next()
            c.dma("sp", sh, hb[:], src[kc * 128:(kc + 1) * 128, ts], reads=[r_src], writes=[rh])
            sq, rsq, _ = self.sqring.next()
            c.op("act", lambda e: e.activation(out=sq[:], in_=hb[:], func=AF.Square), reads=[rh], writes=[rsq])
            c.op("pe", lambda e: e.matmul(pb[:], lhsT=self.ones[:], rhs=sq[:], start=(kc == 0), stop=(kc == KC - 1)),
                 reads=[rsq, self.r_ones], writes=[rpb])
        tb, rt, _ = self.tring.next()
        c.op("dve", lambda e: e.tensor_scalar(out=tb[:], in0=pb[:], scalar1=1.0 / D, scalar2=NORM_EPS, op0=ALU.mult, op1=ALU.add),
             reads=[rpb], writes=[rt])
        tb2, rt2, _ = self.tring.next()
        c.op("act", lambda e: e.activation(out=tb2[:], in_=tb[:], func=AF.Sqrt), reads=[rt], writes=[rt2])
        c.op("dve", lambda e: e.reciprocal(out=self.rstd[:], in_=tb2[:]), reads=[rt2], writes=[self.r_rstd])
        for kc in range(KC):
            hb, rh, sh = self.hring.next()
            c.dma("sp", sh, hb[:], src[kc * 128:(kc + 1) * 128, ts], reads=[r_src], writes=[rh])
            c.op("dve", lambda e: e.scalar_tensor_tensor(out=self.xn[:, kc, :], in0=hb[:], scalar=self.gains[:, gidx, kc:kc + 1],
                                                         in1=self.rstd[:], op0=ALU.mult, op1=ALU.mult),
                 reads=[rh, self.r_rstd, self.r_gains], writes=[self.r_xn])

    def load_w(self, Wt, k0, kcn, n0, ncols):
        c = self.c
        assert n0 % 256 == 0 and ncols <= 256 and k0 % 128 == 0
        wb, rw, sw = self.wring.next()
        src = Wt[n0 // 256, :, k0 // 128:k0 // 128 + kcn, 0:ncols]
        c.dma("pool", sw, wb[:, 0:kcn, 0:ncols], src, writes=[rw])
        return wb, rw

    def ffn(self, src, r_src, dst, r_dst, s_dst, layer, idx):
        c, KC, HC, D, D_FF = self.c, self.KC, self.HC, self.D, self.D_FF
        W1 = self.ffn_w_in[layer, idx]
        W2 = self.ffn_w_out[layer, idx]
        gidx = layer * 3 + (0 if idx == 0 else 2)
        for t in range(self.NT):
            ts = slice(t * TT, (t + 1) * TT)
            self.norm_tile(src, r_src, t, gidx)
            for j0 in range(0, HC, 2):
                nj = min(2, HC - j0)
                wg, rwg = self.load_w(W1, 0, KC, j0 * 128, nj * 128)
                wu, rwu = self.load_w(W1, 0, KC, D_FF + j0 * 128, nj * 128)
                for jj in range(nj):
                    pg, rpg = self.pbank[jj], self.r_pb[jj]
                    for kc in range(KC):
                        c.op("pe", lambda e: e.matmul(pg[:], lhsT=wg[:, kc, jj * 128:(jj + 1) * 128], rhs=self.xn[:, kc, :],
                                                      start=(kc == 0), stop=(kc == KC - 1)), reads=[rwg, self.r_xn], writes=[rpg], sig=(kc == KC - 1))
                    tb, rt, _ = self.tring.next()
                    c.op("act", lambda e: e.activation(out=tb[:], in_=pg[:], func=AF.Silu), reads=[rpg], writes=[rt])
                    pu, rpu = self.pbank[2 + jj], self.r_pb[2 + jj]
                    for kc in range(KC):
                        c.op("pe", lambda e: e.matmul(pu[:], lhsT=wu[:, kc, jj * 128:(jj + 1) * 128], rhs=self.xn[:, kc, :],
                                                      start=(kc == 0), stop=(kc == KC - 1)), reads=[rwu, self.r_xn], writes=[rpu], sig=(kc == KC - 1))
                    c.op("dve", lambda e: e.tensor_tensor(out=self.hid[:, j0 + jj, :], in0=pu[:], in1=tb[:], op=ALU.mult),
                         reads=[rpu, rt], writes=[self.r_hid])
            for jo0 in range(0, KC, 2):
                for kb in range(0, HC, 32):
                    kcn = min(32, HC - kb)
                    w2, rw2 = self.load_w(W2, kb * 128, kcn, jo0 * 128, 256)
                    last_blk = (kb + kcn == HC)
                    for jj in range(2):
                        po, rpo = self.pbank[4 + jj], self.r_pb[4 + jj]
                        for k in range(kcn):
                            c.op("pe", lambda e: e.matmul(po[:], lhsT=w2[:, k, jj * 128:(jj + 1) * 128], rhs=self.hid[:, kb + k, :],
                                                          start=(kb == 0 and k == 0), stop=(last_blk and k == kcn - 1)),
                                 reads=[rw2, self.r_hid], writes=[rpo], sig=(k == kcn - 1))
                for jj in range(2):
                    po, rpo = self.pbank[4 + jj], self.r_pb[4 + jj]
                    jo = jo0 + jj
                    hb, rh, sh = self.hring.next()
                    c.dma("sp", sh, hb[:], src[jo * 128:(jo + 1) * 128, ts], reads=[r_src], writes=[rh])
                    ob, ro, so = self.oring.next()
                    c.op("dve", lambda e: e.scalar_tensor_tensor(out=ob[:], in0=po[:], scalar=0.5, in1=hb[:], op0=ALU.mult, op1=ALU.add),
                         reads=[rpo, rh], writes=[ro])
                    c.dma("sp", s_dst, dst[jo * 128:(jo + 1) * 128, ts], ob[:], reads=[ro], writes=[r_dst])


    def setup_mixer(self):
        c, nc, D, KC = self.c, self.nc, self.D, self.KC
        DEPTH, NTOK = self.DEPTH, self.NTOK
        self.SSMW = D // 4
        self.RH = (3 * D // 8) // 256
        self.RK = self.RH * 128
        self.RW = self.RH * 256
        self.DH = (3 * D // 8) // 128
        self.DK2 = self.DH * 128
        self.DW = self.DH * 128
        self.o_u = 0
        self.o_rq = self.SSMW
        self.o_rk = self.o_rq + self.RK
        self.o_rv = self.o_rk + self.RK
        self.o_rg = self.o_rv + self.RW
        self.o_dq = self.o_rg + self.RW
        self.o_dk = self.o_dq + self.DK2
        self.o_dv = self.o_dk + self.DK2
        self.o_g = self.o_dv + self.DW
        self.N_IN = self.o_g + 3 * D
        self.WTOT = self.SSMW + self.RW + self.DW
        self.mix_w_in = nc.dram_tensor("mix_w_in", [DEPTH, self.N_IN // 256, 128, KC, 256], F32, kind="ExternalInput").ap()
        self.w_b = [nc.dram_tensor("w_branch_ssm", [DEPTH, D // 256, 128, self.SSMW // 128, 256], F32, kind="ExternalInput").ap(),
                    nc.dram_tensor("w_branch_ret", [DEPTH, D // 256, 128, self.RW // 128, 256], F32, kind="ExternalInput").ap(),
                    nc.dram_tensor("w_branch_diff", [DEPTH, D // 256, 128, self.DW // 128, 256], F32, kind="ExternalInput").ap()]
        self.w_o = nc.dram_tensor("w_out", [DEPTH, D // 256, 128, KC, 256], F32, kind="ExternalInput").ap()
        self.dsmall = nc.dram_tensor("dsmall", [DEPTH, 128, 3 + 4 * 64], F32, kind="ExternalInput").ap()
        self.PF_secs = []
        for nm, r0, r1 in (("u", self.o_u, self.o_rq), ("rq", self.o_rq, self.o_rk), ("rk", self.o_rk, self.o_rv), ("rg", self.o_rg, self.o_dq),
                           ("dq", self.o_dq, self.o_dk), ("dk", self.o_dk, self.o_dv)):
            self.PF_secs.append((r0, r1, nc.dram_tensor("PF_" + nm, [r1 - r0, NTOK], F32, kind="Internal").ap()))
        self.VR = nc.dram_tensor("VR", [NTOK, self.RW], F32, kind="Internal").ap()
        self.VD = nc.dram_tensor("VD", [NTOK, self.DW], F32, kind="Internal").ap()
        self.KR = nc.dram_tensor("KR", [NTOK, self.RK], F32, kind="Internal").ap()
        kind = "ExternalOutput" if self.debug else "Internal"
        self.YT = nc.dram_tensor("YT", [self.WTOT, NTOK], F32, kind=kind).ap()
        self.r_PF, self.r_VR, self.r_VD, self.r_KR, self.r_YT = Res("PF"), Res("VR"), Res("VD"), Res("KR"), Res("YT")
        self.s_PF, self.s_VR, self.s_VD, self.s_KR, self.s_YT = (c.newsem() for _ in range(5))
        base = self.hid_off
        S = self.S
        off = [base]

        def at(name, shape, dt, esz):
            n = 1
            for d in shape[1:]:
                n *= d
            t = nc.alloc_sbuf_tensor_at(name, shape, dt, offset=off[0])
            off[0] += ((n * esz + 31) // 32) * 32
            return t
        self.a_q = at("a_q", [128, S], BF16, 2)
        self.a_k = at("a_k", [128, S], BF16, 2)
        self.a_qd = at("a_qd", [128, S], BF16, 2)
        self.a_v = at("a_v", [128, S // 128, 256], BF16, 2)
        self.a_kd = at("a_kd", [128, S // 128, 128], BF16, 2)
        self.a_kf = nc.alloc_sbuf_tensor_at("a_kf", [128, S // 128, 128], F32, offset=self.xn_off)
        assert (S // 128) * 128 * 4 <= self.KC * TT * 2
        self.a_R = at("a_R", [128, 5, TT], F32, 4)
        self.a_dec = at("a_dec", [128, 128], F32, 4)
        self.a_qdt = at("a_qdt", [128, TT], F32, 4)
        self.a_kdt = at("a_kdt", [128, 1], F32, 4)
        self.a_io = at("a_io", [128, TT], F32, 4)
        self.a_pi = at("a_pi", [128, 1], F32, 4)
        self.a_st = at("a_st", [128, 256], F32, 4)
        self.a_stb = at("a_stb", [128, 256], BF16, 2)
        self.a_pt = [at("a_pt%d" % i, [128, TT], BF16, 2) for i in range(3)]
        self.a_ds = at("a_ds", [128, 3 + 4 * 64], F32, 4)
        self.a_lam = at("a_lam", [128, 8], F32, 4)
        self.a_onesb = at("a_onesb", [128, 128], BF16, 2)
        self.a_blk = at("a_blk", [128, 128], F32, 4)
        self.a_i32 = at("a_i32", [128, TT], mybir.dt.int32, 4)
        assert off[0] - base <= self.HC * TT * 2, (off[0] - base, self.HC * TT * 2)
        self.r_ar = Res("arena")
        self.r_tab = Res("tables")
        self.r_pt = [Res("pt%d" % i) for i in range(3)]
        self.r_st, self.r_stb = Res("st"), Res("stb")
        self.pti = 0

    def pf(self, r0, r1):
        for a, b, t in self.PF_secs:
            if a <= r0 and r1 <= b:
                return t[r0 - a:r1 - a, :]
        raise ValueError((r0, r1))

    def mixer_consts(self):
        c = self.c
        I32 = mybir.dt.int32
        r = self.r_tab
        c.op("pool", lambda e: e.iota(self.a_i32[:], pattern=[[1, TT]], base=0, channel_multiplier=-1), writes=[r])
        c.op("dve", lambda e: e.tensor_copy(out=self.a_R[:, 0, :], in_=self.a_i32[:]), reads=[r], writes=[r])
        for j in range(4):
            c.op("pool", lambda e: e.affine_select(out=self.a_R[:, 1 + j, :], in_=self.a_R[:, 0, :], pattern=[[1, TT]],
                                                   compare_op=ALU.is_ge, fill=1.0e6, base=-128 * j, channel_multiplier=-1),
                 reads=[r], writes=[r])
        c.op("pool", lambda e: e.iota(self.a_i32[:], pattern=[[0, TT // 128], [1, 128]], base=0, channel_multiplier=0), reads=[r], writes=[r])
        c.op("dve", lambda e: e.tensor_copy(out=self.a_io[:], in_=self.a_i32[:]), reads=[r], writes=[r])
        c.op("pool", lambda e: e.iota(self.a_i32[:, 0:1], pattern=[[1, 1]], base=0, channel_multiplier=1), reads=[r], writes=[r])
        c.op("dve", lambda e: e.tensor_copy(out=self.a_pi[:], in_=self.a_i32[:, 0:1]), reads=[r], writes=[r])
        c.op("dve", lambda e: e.memset(self.a_onesb[:], 1.0), reads=[r], writes=[r])
        c.op("dve", lambda e: e.memset(self.a_blk[:], 0.0), reads=[r], writes=[r])
        c.op("dve", lambda e: e.memset(self.a_blk[0:64, 0:64], 1.0), reads=[r], writes=[r])
        c.op("dve", lambda e: e.memset(self.a_blk[64:128, 64:128], 1.0), reads=[r], writes=[r])

    def inproj(self, src, r_src, layer):
        c, KC = self.c, self.KC
        W = self.mix_w_in[layer]
        gidx = layer * 3 + 1
        nchunks = self.o_g // 128
        for t in range(self.NT):
            ts = slice(t * TT, (t + 1) * TT)
            self.norm_tile(src, r_src, t, gidx)
            for j0 in range(0, nchunks, 2):
                nj = min(2, nchunks - j0)
                wb, rw = self.load_w(W, 0, KC, j0 * 128, nj * 128)
                for jj in range(nj):
                    col = (j0 + jj) * 128
                    in_rv = self.o_rv <= col < self.o_rg
                    in_dv = self.o_dv <= col < self.o_g
                    in_rk = self.o_rk <= col < self.o_rv
                    if not (in_rv or in_dv):
                        pg, rpg = self.pbank[jj], self.r_pb[jj]
                        for kc in range(KC):
                            c.op("pe", lambda e: e.matmul(pg[:], lhsT=wb[:, kc, jj * 128:(jj + 1) * 128], rhs=self.xn[:, kc, :],
                                                          start=(kc == 0), stop=(kc == KC - 1)), reads=[rw, self.r_xn], writes=[rpg], sig=(kc == KC - 1))
                        ob, ro, so = self.oring.next()
                        c.op("act", lambda e: e.activation(out=ob[:], in_=pg[:], func=AF.Copy), reads=[rpg], writes=[ro])
                        c.dma("sp", self.s_PF, self.pf(col, col + 128)[:, ts], ob[:], reads=[ro], writes=[self.r_PF])
                    if in_rv or in_dv or in_rk:
                        if in_rv:
                            dst, rd, sd, cc = self.VR, self.r_VR, self.s_VR, col - self.o_rv
                        elif in_dv:
                            dst, rd, sd, cc = self.VD, self.r_VD, self.s_VD, col - self.o_dv
                        else:
                            dst, rd, sd, cc = self.KR, self.r_KR, self.s_KR, col - self.o_rk
                        for tb in range(TT // 128):
                            pg, rpg = self.pbank[2 + tb % 2], self.r_pb[2 + tb % 2]
                            for kc in range(KC):
                                c.op("pe", lambda e: e.matmul(pg[:, 0:128], lhsT=self.xn[:, kc, tb * 128:(tb + 1) * 128],
                                                              rhs=wb[:, kc, jj * 128:(jj + 1) * 128], start=(kc == 0), stop=(kc == KC - 1)),
                                     reads=[rw, self.r_xn], writes=[rpg], sig=(kc == KC - 1))
                            tbuf, rt, _ = self.tring.next()
                            c.op("act", lambda e: e.activation(out=tbuf[:, 0:128], in_=pg[:, 0:128], func=AF.Copy), reads=[rpg], writes=[rt])
                            r0 = t * TT + tb * 128
                            c.dma("sp", sd, dst[r0:r0 + 128, cc:cc + 128], tbuf[:, 0:128], reads=[rt], writes=[rd])

    def diff_attn(self, layer):
        c, S = self.c, self.S
        H = self.DH
        nb = self.NTOK // S
        lam_init = 0.8 - 0.6 * math.exp(-0.3 * layer)
        ar, tab = self.r_ar, self.r_tab
        sd = c.newsem()
        ds = self.a_ds
        c.dma("sp", sd, ds[:], self.dsmall[layer], reads=[tab], writes=[tab])
        lam = self.a_lam
        tb, rt, _ = self.tring.next()
        c.op("dve", lambda e: e.scalar_tensor_tensor(out=tb[:, 0:64], in0=ds[:, 3:67], scalar=1.0, in1=ds[:, 67:131], op0=ALU.mult, op1=ALU.mult,
                                                     accum_out=lam[:, 0:1]), reads=[tab], writes=[rt, tab])
        c.op("dve", lambda e: e.scalar_tensor_tensor(out=tb[:, 64:128], in0=ds[:, 131:195], scalar=1.0, in1=ds[:, 195:259], op0=ALU.mult, op1=ALU.mult,
                                                     accum_out=lam[:, 1:2]), reads=[tab], writes=[rt, tab])
        c.op("act", lambda e: e.activation(out=lam[:, 2:4], in_=lam[:, 0:2], func=AF.Exp), reads=[tab], writes=[tab])
        c.op("dve", lambda e: e.tensor_tensor(out=lam[:, 4:5], in0=lam[:, 3:4], in1=lam[:, 2:3], op=ALU.subtract), reads=[tab], writes=[tab])
        c.op("dve", lambda e: e.tensor_scalar(out=lam[:, 5:6], in0=lam[:, 4:5], scalar1=-lam_init, scalar2=None, op0=ALU.add), reads=[tab], writes=[tab])
        c.op("dve", lambda e: e.tensor_scalar(out=lam[:, 6:7], in0=ds[:, 0:1], scalar1=64 ** -0.5, scalar2=None, op0=ALU.mult), reads=[tab], writes=[tab])
        c.op("dve", lambda e: e.tensor_scalar(out=lam[:, 7:8], in0=ds[:, 2:3], scalar1=1.0 - lam_init, scalar2=None, op0=ALU.mult), reads=[tab], writes=[tab])
        NS = S // TT
        for b in range(nb):
            for h in range(H):
                slope = 2.0 ** (-8.0 * (h + 1) / H)
                for which, row0, dstt, gcol in ((0, self.o_dq + h * 128, self.a_q, lam[:, 6:7]), (1, self.o_dk + h * 128, self.a_k, ds[:, 1:2])):
                    for sl in range(NS):
                        cs = slice(b * S + sl * TT, b * S + (sl + 1) * TT)
                        hb, rh, sh = self.hring.next()
                        c.dma("sp", sh, hb[:], self.pf(row0, row0 + 128)[:, cs], reads=[self.r_PF], writes=[rh])
                        sq, rsq, _ = self.sqring.next()
                        c.op("act", lambda e: e.activation(out=sq[:], in_=hb[:], func=AF.Square), reads=[rh], writes=[rsq])
                        pb, rpb = self.pbank[7], self.r_pb[7]
                        c.op("pe", lambda e: e.matmul(pb[:], lhsT=self.a_blk[:], rhs=sq[:], start=True, stop=True), reads=[rsq, tab], writes=[rpb])
                        t1, rt1, _ = self.tring.next()
                        c.op("dve", lambda e: e.tensor_scalar(out=t1[:], in0=pb[:], scalar1=1.0 / 64, scalar2=NORM_EPS, op0=ALU.mult, op1=ALU.add),
                             reads=[rpb], writes=[rt1])
                        t2, rt2, _ = self.tring.next()
                        c.op("act", lambda e: e.activation(out=t2[:], in_=t1[:], func=AF.Sqrt), reads=[rt1], writes=[rt2])
                        c.op("dve", lambda e: e.reciprocal(out=t1[:], in_=t2[:]), reads=[rt2], writes=[rt1])
                        c.op("dve", lambda e: e.scalar_tensor_tensor(out=dstt[:, sl * TT:(sl + 1) * TT], in0=hb[:], scalar=gcol, in1=t1[:],
                                                                     op0=ALU.mult, op1=ALU.mult), reads=[rh, rt1, tab], writes=[ar])
                sv = self.hring.sems[0]
                c.dma("pool", self.s_vload, self.a_v[:, :, 0:128],
                      self.VD[b * S:(b + 1) * S, h * 128:(h + 1) * 128].rearrange("(kb p) e -> p kb e", p=128), reads=[self.r_VD], writes=[ar])
                for qc in range(NS):
                    nkb = 4 * (qc + 1)
                    pn = [self.pbank[2], self.pbank[3]]
                    psm = [self.pbank[4], self.pbank[5]]
                    rpn = [self.r_pb[2], self.r_pb[3]]
                    rps = [self.r_pb[4], self.r_pb[5]]
                    for i in range(2):
                        for kb in range(nkb):
                            ps, rps_ = self.pbank[kb % 2], self.r_pb[kb % 2]
                            c.op("pe", lambda e: e.matmul(ps[:], lhsT=self.a_k[i * 64:(i + 1) * 64, kb * 128:(kb + 1) * 128],
                                                          rhs=self.a_q[i * 64:(i + 1) * 64, qc * TT:(qc + 1) * TT], start=True, stop=True),
                                 reads=[ar], writes=[rps_])
                            j = kb - 4 * qc
                            Rm = self.a_R[:, 0, :] if j < 0 else self.a_R[:, 1 + j, :]
                            delta0 = qc * TT - kb * 128
                            t1, rt1, _ = self.tring.next()
                            c.op("dve", lambda e: e.scalar_tensor_tensor(out=t1[:], in0=Rm, scalar=-slope, in1=ps[:], op0=ALU.mult, op1=ALU.add),
                                 reads=[rps_, tab], writes=[rt1])
                            pt, rpt = self.a_pt[self.pti], self.r_pt[self.pti]
                            self.pti = (self.pti + 1) % 3
                            c.op("act", lambda e: e.activation(out=pt[:], in_=t1[:], func=AF.Exp, bias=float(-slope * delta0), scale=1.0),
                                 reads=[rt1], writes=[rpt])
                            c.op("pe", lambda e: e.matmul(pn[i][:], lhsT=self.a_v[:, kb, 0:128], rhs=pt[:], start=(kb == 0), stop=(kb == nkb - 1)),
                                 reads=[rpt, ar], writes=[rpn[i]], sig=False)
                            c.op("pe", lambda e: e.matmul(psm[i][:], lhsT=self.a_onesb[:], rhs=pt[:], start=(kb == 0), stop=(kb == nkb - 1)),
                                 reads=[rpt, tab], writes=[rps[i]])
                    r0, rr0, _ = self.tring.next()
                    c.op("dve", lambda e: e.reciprocal(out=r0[:], in_=psm[0][:]), reads=[rps[0]], writes=[rr0])
                    a0, ra0, _ = self.oring.next()
                    c.op("dve", lambda e: e.tensor_tensor(out=a0[:], in0=pn[0][:], in1=r0[:], op=ALU.mult), reads=[rpn[0], rr0], writes=[ra0])
                    r1, rr1, _ = self.tring.next()
                    c.op("dve", lambda e: e.reciprocal(out=r1[:], in_=psm[1][:]), reads=[rps[1]], writes=[rr1])
                    a1, ra1, _ = self.oring.next()
                    c.op("dve", lambda e: e.tensor_tensor(out=a1[:], in0=pn[1][:], in1=r1[:], op=ALU.mult), reads=[rpn[1], rr1], writes=[ra1])
                    c.op("dve", lambda e: e.scalar_tensor_tensor(out=a0[:], in0=a1[:], scalar=lam[:, 5:6], in1=a0[:], op0=ALU.mult, op1=ALU.add),
                         reads=[ra1, ra0, tab], writes=[ra0])
                    sq, rsq, _ = self.sqring.next()
                    c.op("act", lambda e: e.activation(out=sq[:], in_=a0[:], func=AF.Square), reads=[ra0], writes=[rsq])
                    pb, rpb = self.pbank[7], self.r_pb[7]
                    c.op("pe", lambda e: e.matmul(pb[:], lhsT=self.ones[:], rhs=sq[:], start=True, stop=True), reads=[rsq, self.r_ones], writes=[rpb])
                    t1, rt1, _ = self.tring.next()
                    c.op("dve", lambda e: e.tensor_scalar(out=t1[:], in0=pb[:], scalar1=1.0 / 128, scalar2=NORM_EPS, op0=ALU.mult, op1=ALU.add),
                         reads=[rpb], writes=[rt1])
                    t2, rt2, _ = self.tring.next()
                    c.op("act", lambda e: e.activation(out=t2[:], in_=t1[:], func=AF.Sqrt), reads=[rt1], writes=[rt2])
                    c.op("dve", lambda e: e.reciprocal(out=t1[:], in_=t2[:]), reads=[rt2], writes=[rt1])
                    c.op("dve", lambda e: e.scalar_tensor_tensor(out=a1[:], in0=a0[:], scalar=lam[:, 7:8], in1=t1[:], op0=ALU.mult, op1=ALU.mult),
                         reads=[ra0, rt1, tab], writes=[ra1])
                    row = self.SSMW + self.RW + h * 128
                    c.dma("sp", self.s_YT, self.YT[row:row + 128, b * S + qc * TT:b * S + (qc + 1) * TT], a1[:], reads=[ra1], writes=[self.r_YT])
    def retention(self, layer):
        c, S = self.c, self.S
        H = self.RH
        nb = self.NTOK // S
        ar, tab = self.r_ar, self.r_tab
        NS = S // TT
        NCH = S // 128
        scale = 128 ** -0.5
        for h in range(H):
            lg = math.log(1.0 - 2.0 ** (-5.0 - h))
            c.op("act", lambda e: e.activation(out=self.a_dec[:], in_=self.a_R[:, 1, 0:128], func=AF.Exp, scale=lg, bias=math.log(scale)),
                 reads=[tab, ar], writes=[ar])
            c.op("act", lambda e: e.activation(out=self.a_qdt[:], in_=self.a_io[:], func=AF.Exp, scale=lg, bias=lg), reads=[tab, ar], writes=[ar])
            c.op("act", lambda e: e.activation(out=self.a_kdt[:], in_=self.a_pi[:], func=AF.Exp, scale=-lg, bias=lg * 127 + math.log(scale)),
                 reads=[tab, ar], writes=[ar])
            cdec = math.exp(lg * 128)
            for b in range(nb):
                for sl in range(NS):
                    cs = slice(b * S + sl * TT, b * S + (sl + 1) * TT)
                    hb, rh, sh = self.hring.next()
                    r0 = self.o_rq + h * 128
                    c.dma("sp", sh, hb[:], self.pf(r0, r0 + 128)[:, cs], reads=[self.r_PF], writes=[rh])
                    c.op("act", lambda e: e.activation(out=self.a_q[:, sl * TT:(sl + 1) * TT], in_=hb[:], func=AF.Copy), reads=[rh], writes=[ar])
                    c.op("dve", lambda e: e.tensor_tensor(out=self.a_qd[:, sl * TT:(sl + 1) * TT], in0=hb[:], in1=self.a_qdt[:], op=ALU.mult),
                         reads=[rh, ar], writes=[ar])
                    hb, rh, sh = self.hring.next()
                    r0 = self.o_rk + h * 128
                    c.dma("sp", sh, hb[:], self.pf(r0, r0 + 128)[:, cs], reads=[self.r_PF], writes=[rh])
                    c.op("act", lambda e: e.activation(out=self.a_k[:, sl * TT:(sl + 1) * TT], in_=hb[:], func=AF.Copy), reads=[rh], writes=[ar])
                c.dma("pool", self.s_vload, self.a_v[:, :, :],
                      self.VR[b * S:(b + 1) * S, h * 256:(h + 1) * 256].rearrange("(kb p) e -> p kb e", p=128), reads=[self.r_VR], writes=[ar])
                c.dma("sp", self.s_kload, self.a_kf[:, :, :],
                      self.KR[b * S:(b + 1) * S, h * 128:(h + 1) * 128].rearrange("(kb p) e -> p kb e", p=128), reads=[self.r_KR], writes=[ar])
                c.op("dve", lambda e: e.tensor_scalar(out=self.a_kd[:, :, :], in0=self.a_kf[:, :, :], scalar1=self.a_kdt[:, 0:1], scalar2=None, op0=ALU.mult),
                     reads=[ar], writes=[ar])
                c.op("dve", lambda e: e.memset(self.a_st[:], 0.0), reads=[self.r_st], writes=[self.r_st])
                c.op("dve", lambda e: e.memset(self.a_stb[:], 0.0), reads=[self.r_stb], writes=[self.r_stb])
                for n in range(NCH):
                    ns = slice(n * 128, (n + 1) * 128)
                    sub = n % 4
                    ps, rps_ = self.pbank[n % 2], self.r_pb[n % 2]
                    c.op("pe", lambda e: e.matmul(ps[:, 0:128], lhsT=self.a_k[:, ns], rhs=self.a_q[:, ns], start=True, stop=True), reads=[ar], writes=[rps_])
                    pt, rpt = self.a_pt[self.pti], self.r_pt[self.pti]
                    self.pti = (self.pti + 1) % 3
                    c.op("dve", lambda e: e.tensor_tensor(out=pt[:, 0:128], in0=ps[:, 0:128], in1=self.a_dec[:], op=ALU.mult), reads=[rps_, ar], writes=[rpt])
                    for ec in range(2):
                        po, rpo = self.pbank[2 + ec], self.r_pb[2 + ec]
                        c.op("pe", lambda e: e.matmul(po[:, sub * 128:(sub + 1) * 128], lhsT=self.a_v[:, n, ec * 128:(ec + 1) * 128], rhs=pt[:, 0:128],
                                                      start=True, stop=False), reads=[rpt, ar], writes=[rpo], sig=False)
                        c.op("pe", lambda e: e.matmul(po[:, sub * 128:(sub + 1) * 128], lhsT=self.a_stb[:, ec * 128:(ec + 1) * 128], rhs=self.a_qd[:, ns],
                                                      start=False, stop=True), reads=[self.r_stb, ar], writes=[rpo])
                    pk, rpk = self.pbank[4], self.r_pb[4]
                    c.op("pe", lambda e: e.matmul(pk[:, 0:256], lhsT=self.a_kd[:, n, :], rhs=self.a_v[:, n, :], start=True, stop=True), reads=[ar], writes=[rpk])
                    c.op("dve", lambda e: e.scalar_tensor_tensor(out=self.a_st[:], in0=self.a_st[:], scalar=cdec, in1=pk[:, 0:256], op0=ALU.mult, op1=ALU.add),
                         reads=[rpk, self.r_st], writes=[self.r_st])
                    c.op("act", lambda e: e.activation(out=self.a_stb[:], in_=self.a_st[:], func=AF.Copy), reads=[self.r_st], writes=[self.r_stb])
                    if sub == 3:
                        sl = n // 4
                        cs = slice(b * S + sl * TT, b * S + (sl + 1) * TT)
                        pb, rpb = self.pbank[7], self.r_pb[7]
                        for ec in range(2):
                            po, rpo = self.pbank[2 + ec], self.r_pb[2 + ec]
                            sq, rsq, _ = self.sqring.next()
                            c.op("act", lambda e: e.activation(out=sq[:], in_=po[:], func=AF.Square), reads=[rpo], writes=[rsq])
                            c.op("pe", lambda e: e.matmul(pb[:], lhsT=self.ones[:], rhs=sq[:], start=(ec == 0), stop=(ec == 1)),
                                 reads=[rsq, self.r_ones], writes=[rpb])
                        t1, rt1, _ = self.tring.next()
                        c.op("dve", lambda e: e.tensor_scalar(out=t1[:], in0=pb[:], scalar1=1.0 / 256, scalar2=NORM_EPS, op0=ALU.mult, op1=ALU.add),
                             reads=[rpb], writes=[rt1])
                        t2, rt2, _ = self.tring.next()
                        c.op("act", lambda e: e.activation(out=t2[:], in_=t1[:], func=AF.Sqrt), reads=[rt1], writes=[rt2])
                        c.op("dve", lambda e: e.reciprocal(out=t1[:], in_=t2[:]), reads=[rt2], writes=[rt1])
                        for ec in range(2):
                            po, rpo = self.pbank[2 + ec], self.r_pb[2 + ec]
                            hb, rh, sh = self.hring.next()
                            r0 = self.o_rg + h * 256 + ec * 128
                            c.dma("sp", sh, hb[:], self.pf(r0, r0 + 128)[:, cs], reads=[self.r_PF], writes=[rh])
                            sg, rsg, _ = self.sqring.next()
                            c.op("act", lambda e: e.activation(out=sg[:], in_=hb[:], func=AF.Silu), reads=[rh], writes=[rsg])
                            ob, ro, so = self.oring.next()
                            c.op("dve", lambda e: e.tensor_tensor(out=ob[:], in0=po[:], in1=t1[:], op=ALU.mult), reads=[rpo, rt1], writes=[ro])
                            c.op("dve", lambda e: e.tensor_tensor(out=ob[:], in0=ob[:], in1=sg[:], op=ALU.mult), reads=[ro, rsg], writes=[ro])
                            row = self.SSMW + h * 256 + ec * 128
                            c.dma("sp", self.s_YT, self.YT[row:row + 128, cs], ob[:], reads=[ro], writes=[self.r_YT])

    def merge(self, src, r_src, dst, r_dst, s_dst, layer):
        c, KC, D = self.c, self.KC, self.D
        W = self.mix_w_in[layer]
        gidx = layer * 3 + 1
        widths = [self.SSMW, self.RW, self.DW]
        yoff = [0, self.SSMW, self.SSMW + self.RW]
        WK = self.WTOT // 128
        ybf = self.hid[:, 0:WK, :]
        mrg = self.hid[:, WK:WK + KC, :]
        rh_ = self.r_hid
        for t in range(self.NT):
            ts = slice(t * TT, (t + 1) * TT)
            self.norm_tile(src, r_src, t, gidx)
            c.dma("pool", self.s_yload, ybf, self.YT[:, ts].rearrange("(kc p) n -> p kc n", p=128), reads=[self.r_YT], writes=[rh_])
            for jo0 in range(0, KC, 2):
                for i in range(3):
                    kcn = widths[i] // 128
                    wbr, rwbr = self.load_w(self.w_b[i][layer], 0, kcn, jo0 * 128, 256)
                    wgt, rwgt = self.load_w(W, 0, KC, self.o_g + i * D + jo0 * 128, 256)
                    for jj in range(2):
                        pg, rpg = self.pbank[jj], self.r_pb[jj]
                        for kc in range(KC):
                            c.op("pe", lambda e: e.matmul(pg[:], lhsT=wgt[:, kc, jj * 128:(jj + 1) * 128], rhs=self.xn[:, kc, :],
                                                          start=(kc == 0), stop=(kc == KC - 1)), reads=[rwgt, self.r_xn], writes=[rpg], sig=(kc == KC - 1))
                        pbp, rpbp = self.pbank[2 + jj], self.r_pb[2 + jj]
                        for kc in range(kcn):
                            c.op("pe", lambda e: e.matmul(pbp[:], lhsT=wbr[:, kc, jj * 128:(jj + 1) * 128], rhs=ybf[:, yoff[i] // 128 + kc, :],
                                                          start=(kc == 0), stop=(kc == kcn - 1)), reads=[rwbr, rh_], writes=[rpbp], sig=(kc == kcn - 1))
                        tb, rt, _ = self.tring.next()
                        c.op("act", lambda e: e.activation(out=tb[:], in_=pg[:], func=AF.Sigmoid), reads=[rpg], writes=[rt])
                        acc, racc = self.macc[jj], self.r_macc[jj]
                        if i == 0:
                            c.op("dve", lambda e: e.tensor_tensor(out=acc[:], in0=pbp[:], in1=tb[:], op=ALU.mult), reads=[rpbp, rt], writes=[racc])
                        else:
                            c.op("dve", lambda e: e.tensor_tensor(out=tb[:], in0=pbp[:], in1=tb[:], op=ALU.mult), reads=[rpbp, rt], writes=[rt])
                            if i == 1:
                                c.op("dve", lambda e: e.tensor_tensor(out=acc[:], in0=acc[:], in1=tb[:], op=ALU.add), reads=[racc, rt], writes=[racc])
                            else:
                                c.op("dve", lambda e: e.tensor_tensor(out=mrg[:, jo0 + jj, :], in0=acc[:], in1=tb[:], op=ALU.add),
                                     reads=[racc, rt], writes=[rh_])
            for jo0 in range(0, KC, 2):
                wo, rwo = self.load_w(self.w_o[layer], 0, KC, jo0 * 128, 256)
                for jj in range(2):
                    po, rpo = self.pbank[4 + jj], self.r_pb[4 + jj]
                    for kc in range(KC):
                        c.op("pe", lambda e: e.matmul(po[:], lhsT=wo[:, kc, jj * 128:(jj + 1) * 128], rhs=mrg[:, kc, :],
                                                      start=(kc == 0), stop=(kc == KC - 1)), reads=[rwo, rh_], writes=[rpo], sig=(kc == KC - 1))
                    jo = jo0 + jj
                    hb, rh, sh = self.hring.next()
                    c.dma("sp", sh, hb[:], src[jo * 128:(jo + 1) * 128, ts], reads=[r_src], writes=[rh])
                    ob, ro, so = self.oring.next()
                    c.op("dve", lambda e: e.tensor_tensor(out=ob[:], in0=po[:], in1=hb[:], op=ALU.add), reads=[rpo, rh], writes=[ro])
                    c.dma("sp", s_dst, dst[jo * 128:(jo + 1) * 128, ts], ob[:], reads=[ro], writes=[r_dst])

    def setup_s5(self):
        c, nc = self.c, self.nc
        DEPTH = self.DEPTH
        self.G = self.SSMW // 16
        self.NP = NP = self.G // 2
        self.CB = CB = self.SSMW // 128
        self.TS = TS = 128
        self.s5p = nc.dram_tensor("s5p", [DEPTH, 128, 3 * NP], F32, kind="ExternalInput").ap()
        self.s5bt = nc.dram_tensor("s5bt", [DEPTH, 2, 128, NP, 128], F32, kind="ExternalInput").ap()
        self.s5ct = nc.dram_tensor("s5ct", [DEPTH, 2, 128, NP, 128], F32, kind="ExternalInput").ap()
        self.s5d = nc.dram_tensor("s5d", [DEPTH, 128, 2 * CB], F32, kind="ExternalInput").ap()
        self.glu_w = nc.dram_tensor("ssm_glu_w", [DEPTH, self.SSMW // 256, 128, self.SSMW // 128, 256], F32, kind="ExternalInput").ap()
        off = [self.hid_off]

        def at(name, shape, dt, esz):
            n = 1
            for d in shape[1:]:
                n *= d
            t = nc.alloc_sbuf_tensor_at(name, shape, dt, offset=off[0])
            off[0] += ((n * esz + 31) // 32) * 32
            return t
        I32 = mybir.dt.int32
        self.s_cos = at("s_cos", [128, NP, TS], F32, 4)
        self.s_sin = at("s_sin", [128, NP, TS], F32, 4)
        self.s_bt = at("s_bt", [128, 2, NP, 128], BF16, 2)
        self.s_cp = at("s_cp", [128, 2, NP, 128], BF16, 2)
        self.s_w = [self.sqring.bufs[0], self.sqring.bufs[1], self.macc[0], self.macc[1]] + [at("s_w%d" % i, [128, TT], F32, 4) for i in range(2)]
        self.s_wi = [at("s_wi%d" % i, [128, TT], I32, 4) for i in range(1)]
        self.s_x = [at("s_x%d" % i, [128, TT], BF16, 2) for i in range(2)]
        self.s_prm = at("s_prm", [128, 3 * NP], F32, 4)
        self.s_sm = at("s_sm", [128, 12, NP], F32, 4)
        self.s_z0 = at("s_z0", [128, 2, NP], F32, 4)
        self.s_dd = at("s_dd", [128, 2 * CB], F32, 4)
        self.s_ri = at("s_ri", [128, TS], F32, 4)
        self.s_tz = at("s_tz", [128, 4], F32, 4)
        assert off[0] - self.hid_off <= self.HC * TT * 2, (off[0] - self.hid_off, self.HC * TT * 2)
        off = [self.xn_off]
        assert 2 * NP * 128 * 4 <= self.KC * TT * 2
        self.s_ctf = at("s_ctf", [128, 2, NP, 128], F32, 4)
        off = [self.xn_off]
        self.s_ge = at("s_ge", [128, CB, TT], F32, 4)
        self.s_geb = at("s_geb", [128, CB, TT], BF16, 2)
        self.s_ub = at("s_ub", [128, CB, TT], BF16, 2)
        assert off[0] - self.xn_off <= self.KC * TT * 2, (off[0] - self.xn_off, self.KC * TT * 2)
        self.r_s5 = Res("s5arena")
        self.r_s5x = Res("s5xn")
        self.r_sx = Res("s5x")
        self.s_s5 = c.newsem()

    def _frac_sin(self, dst, ang, w1, w2, i32, shift):
        c, r = self.c, self.r_s5
        if shift != 0.0:
            c.op("dve", lambda e: e.tensor_scalar(out=w1, in0=ang, scalar1=shift, scalar2=None, op0=ALU.add), reads=[r], writes=[r])
            src = w1
        else:
            src = ang
        c.op("dve", lambda e: e.tensor_copy(out=i32, in_=src), reads=[r], writes=[r])
        c.op("dve", lambda e: e.tensor_copy(out=w2, in_=i32), reads=[r], writes=[r])
        c.op("dve", lambda e: e.tensor_tensor(out=w1, in0=src, in1=w2, op=ALU.subtract), reads=[r], writes=[r])
        c.op("dve", lambda e: e.tensor_scalar(out=w2, in0=w1, scalar1=0.5, scalar2=None, op0=ALU.is_gt), reads=[r], writes=[r])
        c.op("dve", lambda e: e.tensor_tensor(out=w1, in0=w1, in1=w2, op=ALU.subtract), reads=[r], writes=[r])
        c.op("dve", lambda e: e.tensor_scalar(out=w2, in0=w1, scalar1=-0.5, scalar2=None, op0=ALU.is_lt), reads=[r], writes=[r])
        c.op("dve", lambda e: e.tensor_tensor(out=w1, in0=w1, in1=w2, op=ALU.add), reads=[r], writes=[r])
        c.op("act", lambda e: e.activation(out=dst, in_=w1, func=AF.Sin, scale=2.0 * math.pi), reads=[r], writes=[r])

    def s5_prep(self, layer):
        c, r, NP, TS = self.c, self.r_s5, self.NP, self.TS
        sm = self.s_sm
        prm = self.s_prm
        c.dma("sp", self.s_s5, prm[:], self.s5p[layer], reads=[r], writes=[r])
        c.dma("sp", self.s_s5, self.s_dd[:], self.s5d[layer], reads=[r], writes=[r])
        c.dma("pool", self.s_s5, self.s_bt[:], self.s5bt[layer].rearrange("t p n m -> p t n m"), reads=[r], writes=[r])
        c.dma("sp", self.s_s5, self.s_ctf[:], self.s5ct[layer].rearrange("t p n m -> p t n m"), reads=[self.r_s5x, self.r_xn], writes=[self.r_s5x])
        lr, li, ldt = prm[:, 0:NP], prm[:, NP:2 * NP], prm[:, 2 * NP:3 * NP]
        dt, th, mag, f0 = sm[:, 0, :], sm[:, 1, :], sm[:, 2, :], sm[:, 3, :]
        nr, ni, rden, cre, cim, ncim, t1, t2 = (sm[:, k, :] for k in range(4, 12))
        c.op("act", lambda e: e.activation(out=dt, in_=ldt, func=AF.Exp), reads=[r], writes=[r])
        c.op("dve", lambda e: e.tensor_tensor(out=th, in0=li, in1=dt, op=ALU.mult), reads=[r], writes=[r])
        c.op("dve", lambda e: e.tensor_tensor(out=t1, in0=lr, in1=dt, op=ALU.mult), reads=[r], writes=[r])
        c.op("act", lambda e: e.activation(out=mag, in_=t1, func=AF.Exp), reads=[r], writes=[r])
        c.op("dve", lambda e: e.tensor_scalar(out=t1, in0=th, scalar1=1.0 / (2.0 * math.pi), scalar2=None, op0=ALU.mult), reads=[r], writes=[r])
        i32s = self.s_wi[0]
        c.op("dve", lambda e: e.tensor_copy(out=i32s[:, 0:NP], in_=t1), reads=[r], writes=[r])
        c.op("dve", lambda e: e.tensor_copy(out=t2, in_=i32s[:, 0:NP]), reads=[r], writes=[r])
        c.op("dve", lambda e: e.tensor_tensor(out=f0, in0=t1, in1=t2, op=ALU.subtract), reads=[r], writes=[r])
        c.op("pool", lambda e: e.iota(i32s[:, 0:TS], pattern=[[1, TS]], base=1, channel_multiplier=0), reads=[r], writes=[r])
        c.op("dve", lambda e: e.tensor_copy(out=self.s_ri[:], in_=i32s[:, 0:TS]), reads=[r], writes=[r])
        ang, w1, w2 = self.s_w[0], self.s_w[1], self.s_w[2]
        for g0 in range(0, NP, 4):
            for k in range(4):
                gp = g0 + k
                c.op("dve", lambda e: e.tensor_scalar(out=ang[:, k * TS:(k + 1) * TS], in0=self.s_ri[:], scalar1=f0[:, gp:gp + 1], scalar2=None, op0=ALU.mult),
                     reads=[r], writes=[r])
            self._frac_sin(self.s_sin[:, g0:g0 + 4, :].rearrange("p n t -> p (n t)"), ang[:], w1[:], w2[:], i32s[:], 0.0)
            self._frac_sin(self.s_cos[:, g0:g0 + 4, :].rearrange("p n t -> p (n t)"), ang[:], w1[:], w2[:], i32s[:], 0.25)
        c.op("dve", lambda e: e.tensor_tensor(out=nr, in0=mag, in1=self.s_cos[:, :, 0], op=ALU.mult), reads=[r], writes=[r])
        c.op("dve", lambda e: e.tensor_scalar(out=nr, in0=nr, scalar1=-1.0, scalar2=None, op0=ALU.add), reads=[r], writes=[r])
        c.op("dve", lambda e: e.tensor_tensor(out=ni, in0=mag, in1=self.s_sin[:, :, 0], op=ALU.mult), reads=[r], writes=[r])
        c.op("dve", lambda e: e.tensor_tensor(out=t1, in0=lr, in1=lr, op=ALU.mult), reads=[r], writes=[r])
        c.op("dve", lambda e: e.tensor_tensor(out=t2, in0=li, in1=li, op=ALU.mult), reads=[r], writes=[r])
        c.op("dve", lambda e: e.tensor_tensor(out=t1, in0=t1, in1=t2, op=ALU.add), reads=[r], writes=[r])
        c.op("dve", lambda e: e.reciprocal(out=rden, in_=t1), reads=[r], writes=[r])
        c.op("dve", lambda e: e.tensor_tensor(out=t1, in0=nr, in1=lr, op=ALU.mult), reads=[r], writes=[r])
        c.op("dve", lambda e: e.tensor_tensor(out=t2, in0=ni, in1=li, op=ALU.mult), reads=[r], writes=[r])
        c.op("dve", lambda e: e.tensor_tensor(out=t1, in0=t1, in1=t2, op=ALU.add), reads=[r], writes=[r])
        c.op("dve", lambda e: e.tensor_tensor(out=cre, in0=t1, in1=rden, op=ALU.mult), reads=[r], writes=[r])
        c.op("dve", lambda e: e.tensor_tensor(out=t1, in0=ni, in1=lr, op=ALU.mult), reads=[r], writes=[r])
        c.op("dve", lambda e: e.tensor_tensor(out=t2, in0=nr, in1=li, op=ALU.mult), reads=[r], writes=[r])
        c.op("dve", lambda e: e.tensor_tensor(out=t1, in0=t1, in1=t2, op=ALU.subtract), reads=[r], writes=[r])
        c.op("dve", lambda e: e.tensor_tensor(out=cim, in0=t1, in1=rden, op=ALU.mult), reads=[r], writes=[r])
        c.op("dve", lambda e: e.tensor_scalar(out=ncim, in0=cim, scalar1=-1.0, scalar2=None, op0=ALU.mult), reads=[r], writes=[r])
        tw = self.s_w[0]
        for gp in range(NP):
            c.op("dve", lambda e: e.tensor_scalar(out=tw[:, 0:128], in0=self.s_ctf[:, 1, gp, :], scalar1=cim[:, gp:gp + 1], scalar2=None, op0=ALU.mult),
                 reads=[r, self.r_s5x], writes=[r])
            c.op("dve", lambda e: e.scalar_tensor_tensor(out=self.s_cp[:, 0, gp, :], in0=self.s_ctf[:, 0, gp, :], scalar=cre[:, gp:gp + 1], in1=tw[:, 0:128],
                                                         op0=ALU.mult, op1=ALU.subtract), reads=[r, self.r_s5x], writes=[r])
            c.op("dve", lambda e: e.tensor_scalar(out=tw[:, 128:256], in0=self.s_ctf[:, 1, gp, :], scalar1=cre[:, gp:gp + 1], scalar2=None, op0=ALU.mult),
                 reads=[r, self.r_s5x], writes=[r])
            c.op("dve", lambda e: e.scalar_tensor_tensor(out=self.s_cp[:, 1, gp, :], in0=self.s_ctf[:, 0, gp, :], scalar=ncim[:, gp:gp + 1], in1=tw[:, 128:256],
                                                         op0=ALU.mult, op1=ALU.subtract), reads=[r, self.r_s5x], writes=[r])

    def s5(self, layer):
        c, r, rx, NP, TS, CB, S = self.c, self.r_s5, self.r_s5x, self.NP, self.TS, self.CB, self.S
        self.s5_prep(layer)
        nb = self.NTOK // S
        NS = S // TT
        NCK = TT // TS
        mag = self.s_sm[:, 2, :]
        cosb = lambda gp: self.s_cos[:, gp:gp + 1, :].broadcast_to([128, NCK, TS])
        sinb = lambda gp: self.s_sin[:, gp:gp + 1, :].broadcast_to([128, NCK, TS])
        v3 = lambda t: t[:].rearrange("p (n t) -> p n t", t=TS)
        w = self.s_w
        for b in range(nb):
            c.op("dve", lambda e: e.memset(self.s_z0[:], 0.0), reads=[r], writes=[r])
            for sl in range(NS):
                cs = slice(b * S + sl * TT, b * S + (sl + 1) * TT)
                c.dma("pool", self.s_s5, self.s_ub[:], self.pf(self.o_u, self.o_u + self.SSMW)[:, cs].rearrange("(cb p) n -> p cb n", p=128),
                      reads=[self.r_PF, rx], writes=[rx])
                for gp in range(NP):
                    cb = gp // 4
                    p0, rp0 = self.pbank[0], self.r_pb[0]
                    p1, rp1 = self.pbank[1], self.r_pb[1]
                    c.op("pe", lambda e: e.matmul(p0[:], lhsT=self.s_bt[:, 0, gp, :], rhs=self.s_ub[:, cb, :], start=True, stop=True), reads=[r, rx], writes=[rp0])
                    c.op("pe", lambda e: e.matmul(p1[:], lhsT=self.s_bt[:, 1, gp, :], rhs=self.s_ub[:, cb, :], start=True, stop=True), reads=[r, rx], writes=[rp1])
                    c.op("dve", lambda e: e.tensor_tensor(out=v3(w[0]), in0=v3(p0), in1=cosb(gp), op=ALU.mult), reads=[rp0, r], writes=[r])
                    c.op("dve", lambda e: e.tensor_tensor(out=v3(w[1]), in0=v3(p1), in1=sinb(gp), op=ALU.mult), reads=[rp1, r], writes=[r])
                    c.op("dve", lambda e: e.tensor_tensor(out=v3(w[2]), in0=v3(p1), in1=cosb(gp), op=ALU.mult), reads=[rp1, r], writes=[r])
                    c.op("dve", lambda e: e.tensor_tensor(out=v3(w[3]), in0=v3(p0), in1=sinb(gp), op=ALU.mult), reads=[rp0, r], writes=[r])
                    c.op("pool", lambda e: e.tensor_tensor(out=w[0][:], in0=w[0][:], in1=w[1][:], op=ALU.add), reads=[r], writes=[r])
                    c.op("pool", lambda e: e.tensor_tensor(out=w[2][:], in0=w[2][:], in1=w[3][:], op=ALU.subtract), reads=[r], writes=[r])
                    zr, zi = w[4], w[5]
                    ct = self.s_cos[:, gp, TS - 1:TS]
                    st = self.s_sin[:, gp, TS - 1:TS]
                    for n in range(NCK):
                        ns = slice(n * TS, (n + 1) * TS)
                        magb = mag[:, gp:gp + 1].broadcast_to([128, TS])
                        c.op("dve", lambda e: e.tensor_tensor_scan(out=zr[:, ns], data0=magb, data1=w[0][:, ns], initial=self.s_z0[:, 0, gp:gp + 1],
                                                                   op0=ALU.mult, op1=ALU.add), reads=[r], writes=[r])
                        c.op("dve", lambda e: e.tensor_tensor_scan(out=zi[:, ns], data0=magb, data1=w[2][:, ns], initial=self.s_z0[:, 1, gp:gp + 1],
                                                                   op0=ALU.mult, op1=ALU.add), reads=[r], writes=[r])
                        e0 = (n + 1) * TS - 1
                        tz = self.s_tz
                        c.op("dve", lambda e: e.tensor_scalar(out=tz[:, 0:1], in0=zi[:, e0:e0 + 1], scalar1=st, scalar2=None, op0=ALU.mult), reads=[r], writes=[r])
                        c.op("dve", lambda e: e.tensor_scalar(out=tz[:, 1:2], in0=zr[:, e0:e0 + 1], scalar1=st, scalar2=None, op0=ALU.mult), reads=[r], writes=[r])
                        c.op("dve", lambda e: e.scalar_tensor_tensor(out=self.s_z0[:, 0, gp:gp + 1], in0=zr[:, e0:e0 + 1], scalar=ct, in1=tz[:, 0:1],
                                                                     op0=ALU.mult, op1=ALU.subtract), reads=[r], writes=[r])
                        c.op("dve", lambda e: e.scalar_tensor_tensor(out=self.s_z0[:, 1, gp:gp + 1], in0=zi[:, e0:e0 + 1], scalar=ct, in1=tz[:, 1:2],
                                                                     op0=ALU.mult, op1=ALU.add), reads=[r], writes=[r])
                    c.op("pool", lambda e: e.tensor_tensor(out=v3(w[0]), in0=v3(zr), in1=cosb(gp), op=ALU.mult), reads=[r], writes=[r])
                    c.op("pool", lambda e: e.tensor_tensor(out=v3(w[1]), in0=v3(zi), in1=sinb(gp), op=ALU.mult), reads=[r], writes=[r])
                    c.op("pool", lambda e: e.tensor_tensor(out=v3(w[2]), in0=v3(zr), in1=sinb(gp), op=ALU.mult), reads=[r], writes=[r])
                    c.op("pool", lambda e: e.tensor_tensor(out=v3(w[3]), in0=v3(zi), in1=cosb(gp), op=ALU.mult), reads=[r], writes=[r])
                    rxs = self.r_sx
                    c.op("pool", lambda e: e.tensor_tensor(out=self.s_x[0][:], in0=w[0][:], in1=w[1][:], op=ALU.subtract), reads=[r, rxs], writes=[rxs])
                    c.op("pool", lambda e: e.tensor_tensor(out=self.s_x[1][:], in0=w[2][:], in1=w[3][:], op=ALU.add), reads=[r, rxs], writes=[rxs])
                    py, rpy = self.pbank[2], self.r_pb[2]
                    k4 = gp % 4
                    c.op("pe", lambda e: e.matmul(py[:], lhsT=self.s_cp[:, 0, gp, :], rhs=self.s_x[0][:], start=(k4 == 0), stop=False), reads=[r, rxs], writes=[rpy], sig=False)
                    c.op("pe", lambda e: e.matmul(py[:], lhsT=self.s_cp[:, 1, gp, :], rhs=self.s_x[1][:], start=False, stop=(k4 == 3)), reads=[r, rxs], writes=[rpy])
                    if k4 == 3:
                        hb, rh, sh = self.hring.next()
                        r0 = self.o_u + cb * 128
                        c.dma("sp", sh, hb[:], self.pf(r0, r0 + 128)[:, cs], reads=[self.r_PF], writes=[rh])
                        yv, ryv, _ = self.oring.next()
                        c.op("dve", lambda e: e.scalar_tensor_tensor(out=yv[:], in0=hb[:], scalar=self.s_dd[:, cb:cb + 1], in1=py[:], op0=ALU.mult, op1=ALU.add),
                             reads=[rh, rpy, r], writes=[ryv])
                        t1, rt1, _ = self.tring.next()
                        c.op("dve", lambda e: e.tensor_tensor(out=t1[:], in0=yv[:], in1=yv[:], op=ALU.mult), reads=[ryv], writes=[rt1])
                        c.op("dve", lambda e: e.tensor_scalar(out=t1[:], in0=t1[:], scalar1=0.044715, scalar2=1.0, op0=ALU.mult, op1=ALU.add), reads=[rt1], writes=[rt1])
                        c.op("dve", lambda e: e.tensor_tensor(out=t1[:], in0=t1[:], in1=yv[:], op=ALU.mult), reads=[rt1, ryv], writes=[rt1])
                        t2, rt2, _ = self.tring.next()
                        c.op("act", lambda e: e.activation(out=t2[:], in_=t1[:], func=AF.Sigmoid, scale=2.0 * math.sqrt(2.0 / math.pi)), reads=[rt1], writes=[rt2])
                        c.op("dve", lambda e: e.tensor_tensor(out=self.s_ge[:, cb, :], in0=yv[:], in1=t2[:], op=ALU.mult), reads=[ryv, rt2, rx], writes=[rx])
                        c.op("act", lambda e: e.activation(out=self.s_geb[:, cb, :], in_=self.s_ge[:, cb, :], func=AF.Copy), reads=[rx], writes=[rx])
                for ob0 in range(0, CB, 2):
                    wg, rwg = self.load_w(self.glu_w[layer], 0, CB, ob0 * 128, 256)
                    for jj in range(2):
                        ob = ob0 + jj
                        pg, rpg = self.pbank[3 + jj], self.r_pb[3 + jj]
                        for kc in range(CB):
                            c.op("pe", lambda e: e.matmul(pg[:], lhsT=wg[:, kc, jj * 128:(jj + 1) * 128], rhs=self.s_geb[:, kc, :], start=(kc == 0), stop=(kc == CB - 1)),
                                 reads=[rwg, rx], writes=[rpg], sig=(kc == CB - 1))
                        t1, rt1, _ = self.tring.next()
                        c.op("act", lambda e: e.activation(out=t1[:], in_=pg[:], func=AF.Sigmoid, bias=self.s_dd[:, CB + ob:CB + ob + 1], scale=1.0),
                             reads=[rpg, r], writes=[rt1])
                        yo, ryo, _ = self.oring.next()
                        c.op("dve", lambda e: e.tensor_tensor(out=yo[:], in0=self.s_ge[:, ob, :], in1=t1[:], op=ALU.mult), reads=[rx, rt1], writes=[ryo])
                        c.dma("sp", self.s_YT, self.YT[ob * 128:(ob + 1) * 128, cs], yo[:], reads=[ryo], writes=[self.r_YT])


def host_layout_s5(inputs, DEPTH, D):
    f = lambda k: np.asarray(inputs[k], dtype=np.float32)
    G = (D // 4) // 16
    NP = G // 2
    CB = (D // 4) // 128
    P = 64

    def pl(a):
        return a.reshape(DEPTH, NP, 2, P).transpose(0, 2, 3, 1).reshape(DEPTH, 128, NP)
    ldt = np.broadcast_to(f("ssm_log_dt")[:, :, None], (DEPTH, G, P))
    s5p = np.concatenate([pl(f("ssm_lambda_re")), pl(f("ssm_lambda_im")), pl(ldt)], axis=2)
    bt = np.zeros((DEPTH, 2, 128, NP, 128), np.float32)
    ct = np.zeros((DEPTH, 2, 128, NP, 128), np.float32)
    for t, (kb, kc) in enumerate((("ssm_b_re", "ssm_c_re"), ("ssm_b_im", "ssm_c_im"))):
        b = f(kb)
        cc = f(kc)
        for g in range(G):
            gp, gl = g // 2, g % 2
            r0 = (g % 8) * 16
            bt[:, t, r0:r0 + 16, gp, gl * P:(gl + 1) * P] = b[:, g].transpose(0, 2, 1)
            ct[:, t, gl * P:(gl + 1) * P, gp, r0:r0 + 16] = cc[:, g].transpose(0, 2, 1)
    dd = np.concatenate([f("ssm_d").reshape(DEPTH, CB, 128).transpose(0, 2, 1), f("ssm_glu_b").reshape(DEPTH, CB, 128).transpose(0, 2, 1)], axis=2)
    return {"s5p": np.ascontiguousarray(s5p), "s5bt": bt, "s5ct": ct, "s5d": np.ascontiguousarray(dd), "ssm_glu_w": tile_w(f("ssm_glu_w"))}


def build(D, B, S, DEPTH, D_FF, debug=False, nseq=1):
    m = Model(D, B, S, DEPTH, D_FF, debug=debug)
    m.setup_mixer()
    m.setup_s5()
    c = m.c
    bufs = [(m.hA, m.r_hA, m.s_hA), (m.hB, m.r_hB, m.s_hB)]
    cur, r_cur = m.xT, m.r_xT
    bi = 0
    for layer in range(DEPTH):
        dst, r_dst, s_dst = bufs[bi]; bi ^= 1
        m.ffn(cur, r_cur, dst, r_dst, s_dst, layer, 0)
        cur, r_cur = dst, r_dst
        m.inproj(cur, r_cur, layer)
        c.barrier()
        m.mixer_consts()
        m.diff_attn(layer)
        m.retention(layer)
        c.barrier()
        m.s5(layer)
        c.barrier()
        dst, r_dst, s_dst = bufs[bi]; bi ^= 1
        m.merge(cur, r_cur, dst, r_dst, s_dst, layer)
        cur, r_cur = dst, r_dst
        if layer == DEPTH - 1:
            dst, r_dst, s_dst = m.outT, m.r_out, m.s_out
        else:
            dst, r_dst, s_dst = bufs[bi]; bi ^= 1
        m.ffn(cur, r_cur, dst, r_dst, s_dst, layer, 1)
        cur, r_cur = dst, r_dst
    outs = [m.r_out]
    if debug:
        outs.append(m.r_YT)
    c.finish(outs)
    return m


N_CORES = 4


def tile_w(w):
    lead = w.shape[:-2]
    K, N = w.shape[-2:]
    nl = len(lead)
    v = w.reshape(*lead, K // 128, 128, N // 256, 256)
    return np.ascontiguousarray(v.transpose(*range(nl), nl + 2, nl + 1, nl, nl + 3))


def host_layout(inputs, DEPTH, D):
    f = lambda k: np.asarray(inputs[k], dtype=np.float32)
    KC = D // 128
    out = {}
    out["normw"] = np.ascontiguousarray(f("norm_w").reshape(DEPTH * 3, KC, 128).transpose(0, 2, 1))
    ds = np.zeros((DEPTH, 128, 3 + 4 * 64), np.float32)
    ds[:, :, 0] = np.tile(f("diff_q_gain"), (1, 2))
    ds[:, :, 1] = np.tile(f("diff_k_gain"), (1, 2))
    ds[:, :, 2] = f("diff_sub_gain")
    for j, k in enumerate(("diff_lambda_q1", "diff_lambda_k1", "diff_lambda_q2", "diff_lambda_k2")):
        ds[:, :, 3 + 64 * j:3 + 64 * (j + 1)] = f(k)[:, None, :]
    out["dsmall"] = ds
    for k in ("ffn_w_in", "ffn_w_out", "mix_w_in", "w_branch_ssm", "w_branch_ret", "w_branch_diff", "w_out"):
        out[k] = tile_w(f(k))
    out.update(host_layout_s5(inputs, DEPTH, D))
    return out


def kernel(_debug=False, **inputs):
    x = np.asarray(inputs["x"], dtype=np.float32)
    B, S, D = x.shape
    DEPTH = np.asarray(inputs["norm_w"]).shape[0]
    D_FF = np.asarray(inputs["ffn_w_out"]).shape[2]
    ncores = N_CORES if B % N_CORES == 0 else 1
    nseq = B // ncores
    m = build(D, nseq, S, DEPTH, D_FF, debug=_debug, nseq=nseq)
    shared = host_layout(inputs, DEPTH, D)
    xT = x.reshape(B * S, D).T
    per = nseq * S
    in_maps = []
    for i in range(ncores):
        d = dict(shared)
        d["xT"] = np.ascontiguousarray(xT[:, i * per:(i + 1) * per])
        in_maps.append(d)
    res = run_bass_kernel_spmd(m.nc, in_maps, core_ids=list(range(ncores)))
    outT = np.concatenate([r["outT"] for r in res.results], axis=1)
    out = np.ascontiguousarray(outT.T).reshape(B, S, D).astype(np.float32)
    if _debug:
        return out, np.concatenate([r["YT"] for r in res.results], axis=1)
    return out
```
